# Optimizing a Trainium2 kernel written in Bass

```python
import math
import jax
import jax.numpy as jnp
from jax import lax
import numpy as np

D_MODEL = 2048
BATCH = 4
SEQ = 2048
DEPTH = 4
DEC_BATCH = 128
DEC_SEQ = 1
PAST_LEN = 16384
PAGE_SIZE = 128

N_MIXERS = 2
N_S5_LAYERS = (DEPTH + 1) // 2
N_ML_LAYERS = DEPTH // 2
S5_GROUP = 16
S5_GROUPS = D_MODEL // S5_GROUP
S5_STATE = 64
S5_DT_MIN = 1e-3
S5_DT_MAX = 1e-1
ML_HEADS = 4
ML_DV = D_MODEL // ML_HEADS
ML_DK = ML_DV // 2
ML_CHUNK = 128
ML_PROJ_SPLITS = [ML_HEADS * ML_DK, ML_HEADS * ML_DK, ML_HEADS * ML_DV, ML_HEADS * ML_DV, ML_HEADS, ML_HEADS]
ML_PROJ = sum(ML_PROJ_SPLITS)
D_FF = 5504
DN_ALPHA = (2.0 * DEPTH) ** 0.25
DN_BETA = (8.0 * DEPTH) ** -0.25
LN_EPS = 1e-5

kernel_name = 'hybrid_s5_mlstm_macaron_deepnorm_step'


def _layer_norm(x, g, b):
    xf = x.astype(jnp.float32)
    mu = jnp.mean(xf, axis=-1, keepdims=True)
    xc = xf - mu
    var = jnp.mean(xc * xc, axis=-1, keepdims=True)
    return (xc * lax.rsqrt(var + LN_EPS) * g + b).astype(x.dtype)


def _swiglu(x, w_gate, w_up, w_down):
    return (jax.nn.silu(x @ w_gate) * (x @ w_up)) @ w_down


def _complex_affine_combine(e1, e2):
    a1r, a1i, b1r, b1i = e1
    a2r, a2i, b2r, b2i = e2
    return (a2r * a1r - a2i * a1i,
            a2r * a1i + a2i * a1r,
            a2r * b1r - a2i * b1i + b2r,
            a2r * b1i + a2i * b1r + b2i)


def _s5_mix(u, h0_re, h0_im, a_re, a_im, log_dt, b_re, b_im, c_re, c_im, d_skip, w_glu_a, w_glu_b):
    bsz, s, _ = u.shape
    uf = u.astype(jnp.float32)
    ug = uf.reshape(bsz, s, S5_GROUPS, S5_GROUP)
    ar = a_re.astype(jnp.float32)
    ai = a_im.astype(jnp.float32)
    dt = jnp.exp(log_dt.astype(jnp.float32))[:, None]
    mag = jnp.exp(dt * ar)
    abr = mag * jnp.cos(dt * ai)
    abi = mag * jnp.sin(dt * ai)
    zr = abr - 1.0
    zi = abi
    den = ar * ar + ai * ai
    gr = (zr * ar + zi * ai) / den
    gi = (zi * ar - zr * ai) / den
    br = b_re.astype(jnp.float32)
    bi = b_im.astype(jnp.float32)
    bbr = gr[..., None] * br - gi[..., None] * bi
    bbi = gr[..., None] * bi + gi[..., None] * br
    bu_r = jnp.einsum('bsgc,gpc->bsgp', ug, bbr)
    bu_i = jnp.einsum('bsgc,gpc->bsgp', ug, bbi)
    h0r = h0_re.astype(jnp.float32)
    h0i = h0_im.astype(jnp.float32)
    bu_r = bu_r.at[:, 0].add(abr * h0r - abi * h0i)
    bu_i = bu_i.at[:, 0].add(abr * h0i + abi * h0r)
    a_r = jnp.broadcast_to(abr, bu_r.shape)
    a_i = jnp.broadcast_to(abi, bu_i.shape)
    _, _, hr, hi = lax.associative_scan(_complex_affine_combine, (a_r, a_i, bu_r, bu_i), axis=1)
    y = (jnp.einsum('bsgp,gcp->bsgc', hr, c_re.astype(jnp.float32))
         - jnp.einsum('bsgp,gcp->bsgc', hi, c_im.astype(jnp.float32)))
    y = y.reshape(bsz, s, D_MODEL) + d_skip.astype(jnp.float32) * uf
    z = jax.nn.gelu(y).astype(u.dtype)
    out = (z @ w_glu_a) * jax.nn.sigmoid(z @ w_glu_b)
    return out, hr[:, -1], hi[:, -1]


def _to_chunks(t, n_chunks, chunk):
    bsz = t.shape[0]
    t = t.reshape((bsz, n_chunks, chunk) + t.shape[2:])
    perm = (1, 0, 3, 2) + tuple(range(4, t.ndim))
    return jnp.transpose(t, perm)


def _mlstm_mix(x, c0, n0, m0, w_in, b_i, b_f, norm_g, w_out):
    bsz, s, _ = x.shape
    proj = x @ w_in
    cuts = [int(c) for c in np.cumsum(ML_PROJ_SPLITS)[:-1]]
    q, k, v, o_pre, i_pre, f_pre = jnp.split(proj, cuts, axis=-1)
    q = q.reshape(bsz, s, ML_HEADS, ML_DK).astype(jnp.float32)
    k = k.reshape(bsz, s, ML_HEADS, ML_DK).astype(jnp.float32) * (ML_DK ** -0.5)
    v = v.reshape(bsz, s, ML_HEADS, ML_DV).astype(jnp.float32)
    ig = (i_pre + b_i).astype(jnp.float32)
    lf = jax.nn.log_sigmoid((f_pre + b_f).astype(jnp.float32))
    chunk = math.gcd(s, ML_CHUNK)
    n_chunks = s // chunk
    xs = (_to_chunks(q, n_chunks, chunk), _to_chunks(k, n_chunks, chunk), _to_chunks(v, n_chunks, chunk),
          _to_chunks(ig, n_chunks, chunk), _to_chunks(lf, n_chunks, chunk))
    causal = jnp.tril(jnp.ones((chunk, chunk), dtype=bool))

    def step(carry, inp):
        c, n, m = carry
        qc, kc, vc, igc, lfc = inp
        bcum = jnp.cumsum(lfc, axis=-1)
        inter = bcum + m[..., None]
        dmat = bcum[..., :, None] - bcum[..., None, :] + igc[..., None, :]
        dmat = jnp.where(causal, dmat, -jnp.inf)
        mt = jnp.maximum(inter, jnp.max(dmat, axis=-1))
        wts = jnp.exp(dmat - mt[..., None])
        sc = jnp.einsum('bhtd,bhsd->bhts', qc, kc) * wts
        scale = jnp.exp(inter - mt)
        num = scale[..., None] * jnp.einsum('bhtd,bhdv->bhtv', qc, c) + jnp.einsum('bhts,bhsv->bhtv', sc, vc)
        den = scale * jnp.einsum('bhtd,bhd->bht', qc, n) + jnp.sum(sc, axis=-1)
        h = num / jnp.maximum(jnp.abs(den), jnp.exp(-mt))[..., None]
        m_last = mt[..., -1]
        dec = jnp.exp(bcum[..., -1:] - bcum + igc - m_last[..., None])
        carry_scale = jnp.exp(bcum[..., -1] + m - m_last)
        c_new = carry_scale[..., None, None] * c + jnp.einsum('bhs,bhsd,bhsv->bhdv', dec, kc, vc)
        n_new = carry_scale[..., None] * n + jnp.einsum('bhs,bhsd->bhd', dec, kc)
        return (c_new, n_new, m_last), h

    init = (c0.astype(jnp.float32), n0.astype(jnp.float32), m0.astype(jnp.float32))
    (c_f, n_f, m_f), hs = lax.scan(step, init, xs)
    h = jnp.transpose(hs, (1, 0, 3, 2, 4)).reshape(bsz, s, ML_HEADS, ML_DV)
    mu = jnp.mean(h, axis=-1, keepdims=True)
    hc = h - mu
    var = jnp.mean(hc * hc, axis=-1, keepdims=True)
    hn = hc * lax.rsqrt(var + LN_EPS) * norm_g.astype(jnp.float32).reshape(ML_HEADS, ML_DV)
    o = jax.nn.sigmoid(o_pre.astype(jnp.float32)).reshape(bsz, s, ML_HEADS, ML_DV)
    out = (o * hn).reshape(bsz, s, ML_HEADS * ML_DV).astype(x.dtype) @ w_out
    return out, c_f, n_f, m_f


def _trunk(x, s5_re0, s5_im0, ml_c0, ml_n0, ml_m0,
           ln_g, ln_b, ffn_w_gate, ffn_w_up, ffn_w_down,
           s5_a_re, s5_a_im, s5_log_dt, s5_b_re, s5_b_im, s5_c_re, s5_c_im, s5_d, s5_w_a, s5_w_b,
           ml_w_in, ml_b_i, ml_b_f, ml_norm_g, ml_w_out):
    s5_r, s5_i, ml_c, ml_n, ml_m = [], [], [], [], []
    for li in range(DEPTH):
        x = _layer_norm(DN_ALPHA * x + 0.5 * _swiglu(x, ffn_w_gate[li, 0], ffn_w_up[li, 0], ffn_w_down[li, 0]),
                        ln_g[li, 0], ln_b[li, 0])
        j = li // N_MIXERS
        if li % N_MIXERS == 0:
            mix, hr, hi = _s5_mix(x, s5_re0[j], s5_im0[j], s5_a_re[j], s5_a_im[j], s5_log_dt[j],
                                  s5_b_re[j], s5_b_im[j], s5_c_re[j], s5_c_im[j], s5_d[j], s5_w_a[j], s5_w_b[j])
            s5_r.append(hr)
            s5_i.append(hi)
        else:
            mix, c, n, m = _mlstm_mix(x, ml_c0[j], ml_n0[j], ml_m0[j], ml_w_in[j], ml_b_i[j], ml_b_f[j],
                                      ml_norm_g[j], ml_w_out[j])
            ml_c.append(c)
            ml_n.append(n)
            ml_m.append(m)
        x = _layer_norm(DN_ALPHA * x + mix, ln_g[li, 1], ln_b[li, 1])
        x = _layer_norm(DN_ALPHA * x + 0.5 * _swiglu(x, ffn_w_gate[li, 1], ffn_w_up[li, 1], ffn_w_down[li, 1]),
                        ln_g[li, 2], ln_b[li, 2])
    return x, jnp.stack(s5_r), jnp.stack(s5_i), jnp.stack(ml_c), jnp.stack(ml_n), jnp.stack(ml_m)


def setup_inputs(seed: int = 0) -> dict:
    key = jax.random.key(seed)
    ks = iter(jax.random.split(key, 40))
    nrm = lambda shape, scale: jax.random.normal(next(ks), shape, jnp.float32) * scale
    g, p, h = S5_GROUPS, S5_STATE, ML_HEADS
    inp = {}
    inp['x_prompt'] = nrm((BATCH, SEQ, D_MODEL), 1.0)
    inp['x_sample'] = nrm((DEC_BATCH, DEC_SEQ, D_MODEL), 1.0)
    inp['state_s5_re'] = nrm((N_S5_LAYERS, DEC_BATCH, g, p), 0.5)
    inp['state_s5_im'] = nrm((N_S5_LAYERS, DEC_BATCH, g, p), 0.5)
    inp['state_mlstm_C'] = nrm((N_ML_LAYERS, DEC_BATCH, h, ML_DK, ML_DV), ML_DK ** -0.5)
    inp['state_mlstm_n'] = nrm((N_ML_LAYERS, DEC_BATCH, h, ML_DK), ML_DK ** -0.5)
    inp['state_mlstm_m'] = nrm((N_ML_LAYERS, DEC_BATCH, h), 1.0)
    inp['ln_g'] = 1.0 + nrm((DEPTH, 3, D_MODEL), 0.02)
    inp['ln_b'] = nrm((DEPTH, 3, D_MODEL), 0.02)
    inp['ffn_w_gate'] = nrm((DEPTH, 2, D_MODEL, D_FF), D_MODEL ** -0.5)
    inp['ffn_w_up'] = nrm((DEPTH, 2, D_MODEL, D_FF), D_MODEL ** -0.5)
    inp['ffn_w_down'] = nrm((DEPTH, 2, D_FF, D_MODEL), D_FF ** -0.5 * DN_BETA)
    inp['s5_a_re'] = -0.5 + nrm((N_S5_LAYERS, g, p), 0.01)
    inp['s5_a_im'] = jnp.pi * jnp.arange(p, dtype=jnp.float32) + nrm((N_S5_LAYERS, g, p), 0.01)
    inp['s5_log_dt'] = jax.random.uniform(next(ks), (N_S5_LAYERS, g), jnp.float32,
                                          math.log(S5_DT_MIN), math.log(S5_DT_MAX))
    inp['s5_b_re'] = nrm((N_S5_LAYERS, g, p, S5_GROUP), (2.0 * S5_GROUP) ** -0.5)
    inp['s5_b_im'] = nrm((N_S5_LAYERS, g, p, S5_GROUP), (2.0 * S5_GROUP) ** -0.5)
    inp['s5_c_re'] = nrm((N_S5_LAYERS, g, S5_GROUP, p), p ** -0.5)
    inp['s5_c_im'] = nrm((N_S5_LAYERS, g, S5_GROUP, p), p ** -0.5)
    inp['s5_d'] = nrm((N_S5_LAYERS, D_MODEL), 1.0)
    inp['s5_w_a'] = nrm((N_S5_LAYERS, D_MODEL, D_MODEL), D_MODEL ** -0.5 * DN_BETA)
    inp['s5_w_b'] = nrm((N_S5_LAYERS, D_MODEL, D_MODEL), D_MODEL ** -0.5)
    inp['ml_w_in'] = nrm((N_ML_LAYERS, D_MODEL, ML_PROJ), D_MODEL ** -0.5)
    inp['ml_b_i'] = nrm((N_ML_LAYERS, h), 0.1)
    inp['ml_b_f'] = jnp.linspace(3.0, 6.0, h, dtype=jnp.float32) + nrm((N_ML_LAYERS, h), 0.1)
    inp['ml_norm_g'] = 1.0 + nrm((N_ML_LAYERS, h * ML_DV), 0.02)
    inp['ml_w_out'] = nrm((N_ML_LAYERS, h * ML_DV, D_MODEL), (h * ML_DV) ** -0.5 * DN_BETA)
    return inp


def reference(x_prompt, x_sample, state_s5_re, state_s5_im, state_mlstm_C, state_mlstm_n, state_mlstm_m,
              ln_g, ln_b, ffn_w_gate, ffn_w_up, ffn_w_down,
              s5_a_re, s5_a_im, s5_log_dt, s5_b_re, s5_b_im, s5_c_re, s5_c_im, s5_d, s5_w_a, s5_w_b,
              ml_w_in, ml_b_i, ml_b_f, ml_norm_g, ml_w_out):
    weights = (ln_g, ln_b, ffn_w_gate, ffn_w_up, ffn_w_down,
               s5_a_re, s5_a_im, s5_log_dt, s5_b_re, s5_b_im, s5_c_re, s5_c_im, s5_d, s5_w_a, s5_w_b,
               ml_w_in, ml_b_i, ml_b_f, ml_norm_g, ml_w_out)
    bp = x_prompt.shape[0]
    z_s5 = jnp.zeros((N_S5_LAYERS, bp, S5_GROUPS, S5_STATE), jnp.float32)
    z_c = jnp.zeros((N_ML_LAYERS, bp, ML_HEADS, ML_DK, ML_DV), jnp.float32)
    z_n = jnp.zeros((N_ML_LAYERS, bp, ML_HEADS, ML_DK), jnp.float32)
    z_m = jnp.zeros((N_ML_LAYERS, bp, ML_HEADS), jnp.float32)
    y_prompt, p_s5_re, p_s5_im, p_c, p_n, p_m = _trunk(x_prompt, z_s5, z_s5, z_c, z_n, z_m, *weights)
    y_sample, s_s5_re, s_s5_im, s_c, s_n, s_m = _trunk(x_sample, state_s5_re, state_s5_im, state_mlstm_C,
                                                       state_mlstm_n, state_mlstm_m, *weights)
    return (y_prompt, y_sample, p_s5_re, p_s5_im, p_c, p_n, p_m, s_s5_re, s_s5_im, s_c, s_n, s_m)
```

```python
import contextlib
import math
import numpy as np
import concourse.bass as bass
import concourse.mybir as mybir
from concourse.bass_utils import run_bass_kernel_spmd

F32 = mybir.dt.float32
BF16 = mybir.dt.bfloat16
I32 = mybir.dt.int32
AF = mybir.ActivationFunctionType
ALU = mybir.AluOpType
AX = mybir.AxisListType

D = 2048
KT = 16
FF = 5504
MT = 43
SEQ = 2048
NP = 512
NS = 16
DEPTH = 4
NH = 4
DK = 256
DV = 512
G = 128
PS = 64
ALPHA = (2.0 * DEPTH) ** 0.25
EPS = 1e-5
EPS_LN = EPS / (ALPHA * ALPHA)
TWO_PI = 2.0 * math.pi
SIN_SCALE = TWO_PI * (1.0 - 2e-7)
N_CORES = 8

C_ID = 0
C_MASKT = 128
C_IOTA = 256
C_SEL = 768
C_I16 = 1280
C_MG2 = 1296
C_MH = 1298
C_ONES = 1300
C_M96 = 1428
CW = 1429


class Sem:
    def __init__(self, h):
        self.h = h
        self.n = 0


class Reg:
    __slots__ = ("name", "w", "rs")

    def __init__(self, name):
        self.name = name
        self.w = None
        self.rs = {}


class Eng:
    def __init__(self, h, sem, is_pe=False):
        self.h = h
        self.sem = sem
        self.seen = {}
        self.is_pe = is_pe
        self.pending = []


class Bank:
    def __init__(self, ap, reg):
        self.ap = ap
        self.reg = reg


class Builder:
    def __init__(self, nc, es):
        self.nc = nc
        self.es = es
        self.nsem = 0

    def sem(self, name):
        self.nsem += 1
        return Sem(self.es.enter_context(self.nc.semaphore(name)))

    def wait(self, eng, reads, writes):
        need = {}
        for r in reads:
            if r.w is not None:
                s, v = r.w
                if need.get(s, 0) < v:
                    need[s] = v
        for w in writes:
            if w.w is not None:
                s, v = w.w
                if need.get(s, 0) < v:
                    need[s] = v
            for s, v in w.rs.items():
                if need.get(s, 0) < v:
                    need[s] = v
        for s, v in need.items():
            if eng.is_pe and s is eng.sem:
                continue
            if eng.seen.get(s, 0) >= v:
                continue
            eng.h.wait_ge(s.h, v)
            eng.seen[s] = v

    def op(self, eng, fn, reads=(), writes=(), sig=True):
        self.wait(eng, reads, writes)
        ins = fn()
        if sig:
            eng.sem.n += 1
            ins.then_inc(eng.sem.h, 1)
            v = eng.sem.n
            for r in reads:
                r.rs[eng.sem] = v
            for r in eng.pending:
                r.rs[eng.sem] = v
            eng.pending = []
            for w in writes:
                w.w = (eng.sem, v)
                w.rs = {}
        else:
            eng.pending.extend(reads)
        return ins

    def dma(self, q, out, in_, reads, writes, dsem, **kw):
        self.wait(q, reads, writes)
        ins = q.h.dma_start(out=out, in_=in_, **kw)
        dsem.n += 16
        ins.then_inc(dsem.h, 16)
        v = dsem.n
        for r in reads:
            r.rs[dsem] = v
        for w in writes:
            w.w = (dsem, v)
            w.rs = {}


def build_program(dbg=0):
    nc = bass.Bass("TRN2", target_bir_lowering=False)

    def din(name, shape):
        return nc.dram_tensor(name, list(shape), F32, kind="ExternalInput").ap()

    def dout(name, shape):
        return nc.dram_tensor(name, list(shape), F32, kind="ExternalOutput").ap()

    xT_p = din("xT_p", [D, SEQ])
    xT_s = din("xT_s", [D, NS])
    d_lng = din("lng", [128, 192])
    d_lnb = din("lnb", [128, 192])
    w_gate = din("w_gate", [DEPTH, 2, D, FF])
    w_up = din("w_up", [DEPTH, 2, D, FF])
    w_down = din("w_down", [DEPTH, 2, FF, D])
    aE_re = din("aE_re", [2, 128, KT, PS])
    aE_im = din("aE_im", [2, 128, KT, PS])
    ldtE = din("ldtE", [2, 128, KT])
    bE_re = din("bE_re", [2, 128, KT, PS])
    bE_im = din("bE_im", [2, 128, KT, PS])
    aH_re = din("aH_re", [2, 128, 64])
    aH_im = din("aH_im", [2, 128, 64])
    ldtH = din("ldtH", [2, 128, 64])
    cH_re = din("cH_re", [2, 128, 64, 16])
    cH_im = din("cH_im", [2, 128, 64, 16])
    d_s5d = din("s5d", [128, 32])
    s5_wa = din("s5_wa", [2, D, D])
    s5_wb = din("s5_wb", [2, D, D])
    ml_win = din("ml_win", [2, D, 6152])
    ml_wout = din("ml_wout", [2, D, D])
    ml_bif0 = din("ml_bif0", [1, 16])
    ml_bif_tm = din("ml_bif_tm", [2, 16, 8])
    ml_gbc = din("ml_gbc", [2, 128, D])
    st_s5 = din("st_s5", [2, 2, 128, 64, NS])
    st_C = din("st_C", [2, NS, NH, DK, DV])
    st_n = din("st_n", [2, NS, NH * DK])
    st_m = din("st_m", [2, NS, NH])
    d_consts = din("consts", [128, CW])

    sc_gate = nc.dram_tensor("sc_gate", [DEPTH, 2, D, FF], BF16, kind="Internal").ap()
    sc_up = nc.dram_tensor("sc_up", [DEPTH, 2, D, FF], BF16, kind="Internal").ap()
    sc_down = nc.dram_tensor("sc_down", [DEPTH, 2, FF, D], BF16, kind="Internal").ap()
    yT_p = dout("yT_p", [D, SEQ])
    yT_s = dout("yT_s", [D, NS])
    o_p_s5 = dout("o_p_s5", [2, 2, 128, 64])
    o_p_C = dout("o_p_C", [2, NH, DK, DV])
    o_p_n = dout("o_p_n", [2, 128, 8])
    o_p_m = dout("o_p_m", [2, 4, 1])
    o_s_s5 = dout("o_s_s5", [2, 2, 128, 64, NS])
    o_s_C = dout("o_s_C", [2, NS, NH, DK, DV])
    o_s_n = dout("o_s_n", [2, NS, NH * DK])
    o_s_m = dout("o_s_m", [2, NS, NH])

    with contextlib.ExitStack() as es:
        B = Builder(nc, es)

        def sb(name, shape, dt):
            return es.enter_context(nc.sbuf_tensor("sb_" + name, list(shape), dt))

        PE = Eng(nc.tensor, B.sem("pe"), is_pe=True)
        ACT = Eng(nc.scalar, B.sem("act"))
        DVE = Eng(nc.vector, B.sem("dve"))
        SP = Eng(nc.sync, None)
        GQ = Eng(nc.gpsimd, None)

        x32 = sb("x32", [128, KT, NP], F32)
        xb = sb("xb", [128, KT, NP], BF16)
        arena = sb("arena", [128, 32768], BF16)
        AR = Reg("arena")
        wsl = [sb(f"wsl{i}", [128, 8192], BF16) for i in range(3)]
        wsl_reg = [Reg(f"wsl{i}") for i in range(3)]
        wsl_sem = [B.sem(f"wsl{i}") for i in range(3)]
        consts = sb("consts", [128, CW], F32)
        cb16 = sb("cb16", [128, 512], BF16)
        cbias = sb("cbias", [128, 4], F32)
        lng = sb("lng", [128, 192], F32)
        lnb = sb("lnb", [128, 192], F32)
        s5d = sb("s5d", [128, 32], F32)
        hst = sb("hst", [128, 2, 2, 64], F32)
        n32 = sb("n32", [128, 2, 8], F32)
        mrow = sb("mrow", [1, 8], F32)
        mlb0 = sb("mlb0", [1, 16], F32)
        rw = sb("rw", [1, 2048], F32)
        RWr = Reg("rw")
        PCr = [Reg(f"pc{i}") for i in range(8)]
        csem = B.sem("cld")
        reserved = set()
        S5W = Reg("s5w")
        SCR = Reg("scratch_w")
        scsem = B.sem("scst")
        wmode = ["store"]
        CR = Reg("consts")
        x32r = [Reg(f"x32_{k}") for k in range(KT)]
        xbr = [Reg(f"xb_{k}") for k in range(KT)]
        hstR = Reg("hst")
        nR = Reg("n32")
        mR = Reg("mrow")
        ldsem = B.sem("ld")
        stsem = B.sem("st")
        out_sems = [stsem]

        ps_all = es.enter_context(nc.psum_tensor("ps", [128, 8, 512], F32))
        banks = [Bank(ps_all[:, i, :], Reg(f"bank{i}")) for i in range(8)]
        bank_ctr = [0]

        def nb():
            while (bank_ctr[0] % 8) in reserved:
                bank_ctr[0] += 1
            b = banks[bank_ctr[0] % 8]
            bank_ctr[0] += 1
            return b

        slot_ctr = [0]

        def next_slot():
            i = slot_ctr[0] % 3
            slot_ctr[0] += 1
            return wsl[i], wsl_reg[i], wsl_sem[i]

        def av(off, n, dt=F32):
            if dt == F32:
                return arena[:, off // 2: off // 2 + 2 * n].bitcast(F32)
            if dt == I32:
                return arena[:, off // 2: off // 2 + 2 * n].bitcast(I32)
            return arena[:, off // 2: off // 2 + n]

        def V_tt(out, a, b, op, reads, writes):
            return B.op(DVE, lambda: nc.vector.tensor_tensor(out=out, in0=a, in1=b, op=op), reads, writes)

        def V_ts(out, a, s1, op0, reads, writes, s2=None, op1=None):
            if op1 is None:
                return B.op(DVE, lambda: nc.vector.tensor_scalar(out=out, in0=a, scalar1=s1, scalar2=None, op0=op0), reads, writes)
            return B.op(DVE, lambda: nc.vector.tensor_scalar(out=out, in0=a, scalar1=s1, scalar2=s2, op0=op0, op1=op1), reads, writes)

        def V_stt(out, a, s, b, op0, op1, reads, writes):
            return B.op(DVE, lambda: nc.vector.scalar_tensor_tensor(out=out, in0=a, scalar=s, in1=b, op0=op0, op1=op1), reads, writes)

        def V_cp(out, a, reads, writes):
            return B.op(DVE, lambda: nc.vector.tensor_copy(out=out, in_=a), reads, writes)

        def A_f(out, a, func, reads, writes, bias=None, scale=None):
            kw = {}
            if bias is not None:
                kw["bias"] = bias
            if scale is not None:
                kw["scale"] = scale
            return B.op(ACT, lambda: nc.scalar.activation(out=out, in_=a, func=func, **kw), reads, writes)

        def MM(bank, out, lhsT, rhs, reads, start, stop):
            return B.op(PE, lambda: nc.tensor.matmul(out, lhsT, rhs, start=start, stop=stop),
                        reads, [bank.reg], sig=True)

        def TR(bank, out, in_, ident, reads):
            return B.op(PE, lambda: nc.tensor.transpose(out, in_, ident), reads, [bank.reg], sig=True)

        B.dma(SP, consts[:], d_consts[:, :], [], [CR], ldsem)
        B.dma(SP, lng[:], d_lng[:, :], [], [CR], ldsem)
        B.dma(SP, lnb[:], d_lnb[:, :], [], [CR], ldsem)
        B.dma(SP, s5d[:], d_s5d[:, :], [], [CR], ldsem)
        B.dma(SP, mlb0[:], ml_bif0[:, :], [], [CR], ldsem)
        V_cp(cb16[:, 0:128], consts[:, C_ID:C_ID + 128], [CR], [CR])
        V_cp(cb16[:, 128:256], consts[:, C_ONES:C_ONES + 128], [CR], [CR])
        B.op(DVE, lambda: nc.vector.memset(cbias[:, 0:1], math.pi / 2.0), [], [CR])
        B.op(DVE, lambda: nc.vector.memset(cbias[:, 1:2], EPS_LN), [], [CR])
        B.op(DVE, lambda: nc.vector.memset(cbias[:, 2:3], EPS), [], [CR])
        B.op(DVE, lambda: nc.vector.memset(cbias[:, 3:4], 1.0), [], [CR])
        B.op(DVE, lambda: nc.vector.memset(hst[:], 0.0), [], [hstR])
        B.op(DVE, lambda: nc.vector.memset(n32[:], 0.0), [], [nR])
        B.op(DVE, lambda: nc.vector.memset(mrow[:], 0.0), [], [mR])
        ident32 = consts[:, C_ID:C_ID + 128]
        identb = cb16[:, 0:128]
        onesb = cb16[:, 128:256]
        maskT = consts[:, C_MASKT:C_MASKT + 128]
        iota1 = consts[:, C_IOTA:C_IOTA + 512]
        I16 = consts[0:16, C_I16:C_I16 + 16]
        ones32 = consts[:, C_ONES:C_ONES + 128]

        def stream_fm(mats, ncols, N, src, evac, cw=256, scr=None):
            nm = len(mats)
            c0 = 0
            while c0 < ncols:
                w = min(cw, ncols - c0)
                st, sr, ss = next_slot()
                views = []
                for j, m in enumerate(mats):
                    v = st[:, j * 4096: j * 4096 + KT * w].rearrange("p (k c) -> p k c", k=KT)
                    if scr is not None and wmode[0] == "load":
                        B.dma(GQ, v, scr[j][:, c0:c0 + w].rearrange("(k p) c -> p k c", p=128), [SCR], [sr], ss)
                    else:
                        B.dma(GQ, v, m[:, c0:c0 + w].rearrange("(k p) c -> p k c", p=128), [], [sr], ss)
                        if scr is not None:
                            B.dma(SP, scr[j][:, c0:c0 + w].rearrange("(k p) c -> p k c", p=128), v, [sr], [SCR], scsem)
                    views.append(v)
                for mi in range(w // 128):
                    bs = []
                    for j in range(nm):
                        bk = nb()
                        for kt in range(KT):
                            sap, sreg = src(kt)
                            MM(bk, bk.ap[:, :N], views[j][:, kt, mi * 128:(mi + 1) * 128], sap, [sr, sreg],
                               kt == 0, kt == KT - 1)
                        bs.append(bk)
                    evac(c0 // 128 + mi, bs)
                c0 += w

        def stream_tm(mat, c_lo, nchunks, tchunks, evac):
            for cc in range(nchunks):
                st, sr, ss = next_slot()
                v = st[:, 0:KT * 512].rearrange("p (k c) -> p k c", k=KT)
                B.dma(GQ, v, mat[:, c_lo + cc * 512: c_lo + (cc + 1) * 512].rearrange("(k p) c -> p k c", p=128),
                      [], [sr], ss)
                for ti, (t0, nt) in enumerate(tchunks):
                    bk = nb()
                    for kt in range(KT):
                        MM(bk, bk.ap[:nt, :], xb[:, kt, t0:t0 + nt], v[:, kt, :], [sr, xbr[kt]], kt == 0, kt == KT - 1)
                    evac(cc, ti, bk)

        def src_xb(N):
            return lambda kt: (xb[:, kt, :N], xbr[kt])

        def layer_norm(l, i, N):
            zb = av(0, KT * NP, BF16).rearrange("p (k n) -> p k n", k=KT)
            z2b = av(16384, KT * NP, BF16).rearrange("p (k n) -> p k n", k=KT)
            mean = av(32768, NP)
            msq = av(32768 + 2048, NP)
            rstd = av(32768 + 4096, NP)
            nmr = av(32768 + 6144, NP)
            A_f(zb[:, :, :N], x32[:, :, :N], AF.Copy, x32r, [AR])
            A_f(z2b[:, :, :N], x32[:, :, :N], AF.Square, x32r, [AR])
            b1 = nb()
            b2 = nb()
            for kt in range(KT):
                MM(b1, b1.ap[:, :N], onesb, zb[:, kt, :N], [AR, CR], kt == 0, kt == KT - 1)
            for kt in range(KT):
                MM(b2, b2.ap[:, :N], onesb, z2b[:, kt, :N], [AR, CR], kt == 0, kt == KT - 1)
            V_ts(mean[:, :N], b1.ap[:, :N], 1.0 / D, ALU.mult, [b1.reg], [AR])
            V_tt(msq[:, :N], mean[:, :N], mean[:, :N], ALU.mult, [AR], [AR])
            V_stt(rstd[:, :N], b2.ap[:, :N], 1.0 / D, msq[:, :N], ALU.mult, ALU.subtract, [b2.reg, AR], [AR])
            A_f(rstd[:, :N], rstd[:, :N], AF.Sqrt, [AR, CR], [AR], bias=cbias[:, 1:2])
            B.op(DVE, lambda: nc.vector.reciprocal(out=rstd[:, :N], in_=rstd[:, :N]), [AR], [AR])
            V_cp(b1.ap[:, :N], rstd[:, :N], [AR, b1.reg], [b1.reg])
            V_stt(b2.ap[:, :N], mean[:, :N], -1.0, rstd[:, :N], ALU.mult, ALU.mult, [AR, b2.reg], [b2.reg])
            gi = (l * 3 + i) * KT
            V_tt(x32[:, :, :N], x32[:, :, :N], b1.ap[:, :N].unsqueeze(1).to_broadcast([128, KT, N]), ALU.mult,
                 x32r + [b1.reg], x32r)
            V_tt(x32[:, :, :N], x32[:, :, :N], b2.ap[:, :N].unsqueeze(1).to_broadcast([128, KT, N]), ALU.add,
                 x32r + [b2.reg], x32r)
            for kt in range(KT):
                A_f(x32[:, kt, :N], x32[:, kt, :N], AF.Identity, [x32r[kt], CR], [x32r[kt]],
                    bias=lnb[:, gi + kt:gi + kt + 1], scale=lng[:, gi + kt:gi + kt + 1])
            A_f(xb[:, :, :N], x32[:, :, :N], AF.Copy, x32r, xbr)

        def ffn_s(l, i):
            N = NS
            R = [AR]
            g_sb = av(0, FF)[0:16]
            h_tm = av(22528, FF, BF16)[0:16]
            hT = av(34816, MT * NS, BF16).rearrange("p (m n) -> p m n", m=MT)
            y_tm = av(36864, D)[0:16]
            chunks = [(c0, min(512, FF - c0)) for c0 in range(0, FF, 512)]
            for which, wmat in enumerate((w_gate, w_up)):
                for (c0, w) in chunks:
                    st, sr, ss = next_slot()
                    v = st[:, 0:KT * w].rearrange("p (k c) -> p k c", k=KT)
                    scm = (sc_gate, sc_up)[which]
                    if wmode[0] == "load":
                        B.dma(GQ, v, scm[l, i][:, c0:c0 + w].rearrange("(k p) c -> p k c", p=128), [SCR], [sr], ss)
                    else:
                        B.dma(GQ, v, wmat[l, i][:, c0:c0 + w].rearrange("(k p) c -> p k c", p=128), [], [sr], ss)
                    bk = nb()
                    for kt in range(KT):
                        MM(bk, bk.ap[0:16, 0:w], xb[:, kt, 0:16], v[:, kt, :], [sr, xbr[kt]], kt == 0, kt == KT - 1)
                    if which == 0:
                        A_f(g_sb[:, c0:c0 + w], bk.ap[0:16, 0:w], AF.Silu, [bk.reg] + R, R)
                    else:
                        V_tt(h_tm[:, c0:c0 + w], g_sb[:, c0:c0 + w], bk.ap[0:16, 0:w], ALU.mult, [bk.reg] + R, R)
            bt = nb()
            btb = bt.ap.bitcast(BF16)
            for m in range(MT):
                TR(bt, btb[:, m * 16:(m + 1) * 16], h_tm[:, m * 128:(m + 1) * 128], identb[0:16, 0:16], R + [CR])
            A_f(hT, btb[:, 0:MT * 16].rearrange("p (m n) -> p m n", m=MT), AF.Copy, [bt.reg] + R, R)
            fchunks = [(f0, min(8, MT - f0)) for f0 in range(0, MT, 8)]
            for dg in range(4):
                bk = nb()
                for (f0, nf) in fchunks:
                    st, sr, ss = next_slot()
                    v = st[:, 0:nf * 512].rearrange("p (f c) -> p f c", f=nf)
                    if wmode[0] == "load":
                        B.dma(GQ, v, sc_down[l, i, f0 * 128:(f0 + nf) * 128, dg * 512:(dg + 1) * 512]
                              .rearrange("(f p) c -> p f c", p=128), [SCR], [sr], ss)
                    else:
                        B.dma(GQ, v, w_down[l, i, f0 * 128:(f0 + nf) * 128, dg * 512:(dg + 1) * 512]
                              .rearrange("(f p) c -> p f c", p=128), [], [sr], ss)
                    for fi in range(nf):
                        f = f0 + fi
                        MM(bk, bk.ap[0:16, :], hT[:, f, :], v[:, fi, :], [sr, AR], f == 0, f == MT - 1)
                A_f(y_tm[:, dg * 512:(dg + 1) * 512], bk.ap[0:16, :], AF.Copy, [bk.reg] + R, R)
            by = nb()
            for kt in range(KT):
                TR(by, by.ap[:, kt * 16:(kt + 1) * 16], y_tm[:, kt * 128:(kt + 1) * 128], ident32[0:16, 0:16], R + [CR])
            V_stt(x32[:, :, :N], by.ap[:, 0:KT * 16].rearrange("p (k n) -> p k n", k=KT), 0.5 / ALPHA, x32[:, :, :N],
                  ALU.mult, ALU.add, [by.reg] + x32r, x32r)
            layer_norm(l, 2 * i, N)

        def ffn(l, i, N):
            if N == NS:
                return ffn_s(l, i)
            hT = av(0, MT * NP, BF16).rearrange("p (m n) -> p m n", m=MT)
            hTr = [Reg(f"hT{m}") for m in range(MT)]
            sgs = [av(44032 + 2048 * j, NP) for j in range(2)]
            sgr = [Reg("sg0"), Reg("sg1")]
            for m in range(MT):
                hTr[m].w = AR.w
                hTr[m].rs = dict(AR.rs)
            for j in range(2):
                sgr[j].w = AR.w
                sgr[j].rs = dict(AR.rs)
            cnt = [0]

            def evac(m, bs):
                j = cnt[0] % 2
                cnt[0] += 1
                A_f(sgs[j][:, :N], bs[0].ap[:, :N], AF.Silu, [bs[0].reg], [sgr[j]])
                V_tt(hT[:, m, :N], sgs[j][:, :N], bs[1].ap[:, :N], ALU.mult, [sgr[j], bs[1].reg], [hTr[m]])

            stream_fm([w_gate[l, i], w_up[l, i]], FF, N, src_xb(N), evac, scr=[sc_gate[l, i], sc_up[l, i]])
            fchunks = [(f0, min(8, MT - f0)) for f0 in range(0, MT, 8)]
            for dg in range(4):
                bks = [nb() for _ in range(4)]
                for (f0, nf) in fchunks:
                    st, sr, ss = next_slot()
                    v = st[:, 0:nf * 512].rearrange("p (f c) -> p f c", f=nf)
                    scv = sc_down[l, i, f0 * 128:(f0 + nf) * 128, dg * 512:(dg + 1) * 512].rearrange("(f p) c -> p f c", p=128)
                    if wmode[0] == "load":
                        B.dma(GQ, v, scv, [SCR], [sr], ss)
                    else:
                        B.dma(GQ, v, w_down[l, i, f0 * 128:(f0 + nf) * 128, dg * 512:(dg + 1) * 512]
                              .rearrange("(f p) c -> p f c", p=128), [], [sr], ss)
                        B.dma(SP, scv, v, [sr], [SCR], scsem)
                    for fi in range(nf):
                        f = f0 + fi
                        for dt in range(4):
                            MM(bks[dt], bks[dt].ap[:, :N], v[:, fi, dt * 128:(dt + 1) * 128], hT[:, f, :N],
                               [sr, hTr[f]], f == 0, f == MT - 1)
                for dt in range(4):
                    kt = dg * 4 + dt
                    V_stt(x32[:, kt, :N], bks[dt].ap[:, :N], 0.5 / ALPHA, x32[:, kt, :N], ALU.mult, ALU.add,
                          [bks[dt].reg, x32r[kt]], [x32r[kt]])
            collect_arena(hTr + sgr)
            layer_norm(l, 2 * i, N)

        def collect_arena(regs):
            ws = {}
            for r in regs:
                if r.w is not None:
                    s, v = r.w
                    ws[s] = max(ws.get(s, 0), v)
                for s, v in r.rs.items():
                    ws[s] = max(ws.get(s, 0), v)
            if AR.w is not None:
                s, v = AR.w
                ws[s] = max(ws.get(s, 0), v)
            for s, v in ws.items():
                AR.rs[s] = max(AR.rs.get(s, 0), v)

        def sincos(th, n_el, shape_fn, sn, cs, ki, fr, regs):
            V_cp(ki, th, regs, regs)
            V_tt(fr, th, ki, ALU.subtract, regs, regs)
            A_f(sn, fr, AF.Sin, regs, regs, scale=SIN_SCALE)
            A_f(fr, fr, AF.Abs, regs, regs)
            A_f(cs, fr, AF.Sin, regs + [CR], regs, scale=-SIN_SCALE, bias=cbias[:, 0:1])

        def s5_mix(l, j, N, is_s):
            R = [AR]
            Bblk = av(0, KT * 2 * 128, BF16).rearrange("p (k r m) -> p k r m", k=KT, r=2)
            Cblk = av(8192, 64 * 2 * 32, BF16).rearrange("p (i r m) -> p i r m", i=64, r=2)
            zb = av(16384, KT * NP, BF16).rearrange("p (k n) -> p k n", k=KT)
            thn = av(32768, 64)
            rho = av(32768 + 256, 64)
            abr = av(32768 + 512, 64)
            abi = av(32768 + 768, 64)
            nabi = av(32768 + 1024, 64)
            P0 = 34816
            eoffs = [16384, 20480, 24576, 28672, P0, P0 + 4096, P0 + 8192, P0 + 12288]

            def e(k, dt=F32):
                return av(eoffs[k], KT * PS, dt).rearrange("p (k s) -> p k s", k=KT)
            ar_, ai_, t0, t1, t2, t3, t4 = e(0), e(1), e(2), e(3), e(4), e(5), e(6)
            kiE = e(7, I32)
            dtE = av(P0 + 16384, KT)
            B.dma(SP, ar_, aE_re[j], [], R, ldsem)
            B.dma(SP, ai_, aE_im[j], [], R, ldsem)
            B.dma(SP, dtE, ldtE[j], [], R, ldsem)
            A_f(dtE, dtE, AF.Exp, R, R)
            dtb = dtE.unsqueeze(2).to_broadcast([128, KT, PS])
            V_tt(t0, ar_, dtb, ALU.mult, R, R)
            A_f(t0, t0, AF.Exp, R, R)
            V_tt(t1, ai_, dtb, ALU.mult, R, R)
            V_ts(t1, t1, 1.0 / TWO_PI, ALU.mult, R, R)
            sincos(t1, None, None, t2, t3, kiE, t4, R)
            V_tt(t2, t2, t0, ALU.mult, R, R)
            V_tt(t3, t3, t0, ALU.mult, R, R)
            V_ts(t3, t3, -1.0, ALU.add, R, R)
            V_tt(t0, ar_, ar_, ALU.mult, R, R)
            V_tt(t1, ai_, ai_, ALU.mult, R, R)
            V_tt(t0, t0, t1, ALU.add, R, R)
            B.op(DVE, lambda: nc.vector.reciprocal(out=t0, in_=t0), R, R)
            V_tt(t1, t3, ar_, ALU.mult, R, R)
            V_tt(t4, t2, ai_, ALU.mult, R, R)
            V_tt(t1, t1, t4, ALU.add, R, R)
            V_tt(t1, t1, t0, ALU.mult, R, R)
            V_tt(t4, t2, ar_, ALU.mult, R, R)
            V_tt(t3, t3, ai_, ALU.mult, R, R)
            V_tt(t4, t4, t3, ALU.subtract, R, R)
            V_tt(t4, t4, t0, ALU.mult, R, R)
            B.dma(SP, ar_, bE_re[j], [], R, ldsem)
            B.dma(SP, ai_, bE_im[j], [], R, ldsem)
            V_tt(t0, t1, ar_, ALU.mult, R, R)
            V_tt(t2, t4, ai_, ALU.mult, R, R)
            V_tt(t0, t0, t2, ALU.subtract, R, R)
            V_tt(t2, t1, ai_, ALU.mult, R, R)
            V_tt(t3, t4, ar_, ALU.mult, R, R)
            V_tt(t2, t2, t3, ALU.add, R, R)
            for g2 in range(2):
                V_ts(Bblk[:, :, 0, g2 * 64:(g2 + 1) * 64], t0, consts[:, C_MG2 + g2:C_MG2 + g2 + 1], ALU.mult, R + [CR], R + [S5W])
                V_ts(Bblk[:, :, 1, g2 * 64:(g2 + 1) * 64], t2, consts[:, C_MG2 + g2:C_MG2 + g2 + 1], ALU.mult, R + [CR], R + [S5W])
            hA = av(P0, 64)
            hB = av(P0 + 256, 64)
            hC = av(P0 + 512, 64)
            hD = av(P0 + 768, 64)
            hK = av(P0 + 1024, 64, I32)
            cT = av(P0 + 2048, 64 * 16).rearrange("p (i c) -> p i c", i=64)
            B.dma(SP, hA, aH_re[j], [], R, ldsem)
            B.dma(SP, hB, aH_im[j], [], R, ldsem)
            B.dma(SP, hC, ldtH[j], [], R, ldsem)
            A_f(hC, hC, AF.Exp, R, R)
            V_tt(rho, hA, hC, ALU.mult, R, R)
            A_f(rho, rho, AF.Exp, R, R)
            V_tt(hB, hB, hC, ALU.mult, R, R)
            V_ts(hB, hB, 1.0 / TWO_PI, ALU.mult, R, R)
            V_cp(hK, hB, R, R)
            V_tt(thn, hB, hK, ALU.subtract, R, R)
            A_f(hA, thn, AF.Sin, R, R, scale=SIN_SCALE)
            A_f(hD, thn, AF.Abs, R, R)
            A_f(hD, hD, AF.Sin, R + [CR], R, scale=-SIN_SCALE, bias=cbias[:, 0:1])
            V_tt(abi, hA, rho, ALU.mult, R, R)
            V_tt(abr, hD, rho, ALU.mult, R, R)
            V_ts(nabi, abi, -1.0, ALU.mult, R, R)
            B.dma(SP, cT, cH_re[j], [], R, ldsem)
            for g2 in range(2):
                V_ts(Cblk[:, :, 0, g2 * 16:(g2 + 1) * 16], cT, consts[:, C_MH + g2:C_MH + g2 + 1], ALU.mult, R + [CR], R + [S5W])
            B.dma(SP, cT, cH_im[j], [], R, ldsem)
            for g2 in range(2):
                V_ts(Cblk[:, :, 1, g2 * 16:(g2 + 1) * 16], cT, consts[:, C_MH + g2:C_MH + g2 + 1], ALU.mult, R + [CR], R + [S5W],
                     s2=-1.0, op1=ALU.mult)
            kf = av(P0, NP)
            kfi = av(P0, NP, I32)
            sn = av(P0 + 2048, NP)
            cs = av(P0 + 4096, NP)
            u1 = av(P0 + 6144, NP)
            u2 = av(P0 + 8192, NP)
            gr = av(P0 + 10240, NP)
            gi = av(P0 + 12288, NP)
            ysb = gi
            Cq3 = av(49152, KT * 2 * 64, BF16).rearrange("p (k r m) -> p k r m", k=KT, r=2)
            B.op(DVE, lambda: nc.vector.memset(Cq3, 0.0), R, R + [S5W])
            for kt_ in range(KT):
                V_cp(Cq3[:, kt_, :, 32:64], Cblk[:, 4 * kt_ + 3, :, :], R, R + [S5W])
            HrA = [av(53248, NP, BF16), av(55296, NP, BF16)]
            HiA = [av(54272, NP, BF16), av(56320, NP, BF16)]
            Bq3 = av(57344, KT * 2 * 128, BF16).rearrange("p (k r m) -> p k r m", k=KT, r=2)
            V_ts(Bq3, Bblk, consts[:, C_M96:C_M96 + 1], ALU.mult, R + [CR], R + [S5W])
            Rt, Ru, Ry, Rz = Reg("s5t"), Reg("s5u"), Reg("s5y"), Reg("s5z")
            RH = [Reg("s5h0"), Reg("s5h1")]
            for r_ in [Rt, Ru, Ry, Rz] + RH:
                r_.w = AR.w
                r_.rs = dict(AR.rs)
            if is_s:
                u1 = av(P0, NS)
                u2 = av(P0 + 64, NS)
                zb = av(16384, KT * NS, BF16).rearrange("p (k n) -> p k n", k=KT)
                h0 = av(16896, 2 * 64 * NS).rearrange("p (r i b) -> p r i b", r=2, i=64)
                B.dma(SP, h0[:, 0], st_s5[j, 0], [], [Ru], ldsem)
                B.dma(SP, h0[:, 1], st_s5[j, 1], [], [Ru], ldsem)
                hn_ = av(34944, 2 * 64 * NS).rearrange("p (r i b) -> p r i b", r=2, i=64)
            ybank = None

            def emitB(ii_):
                kt_b = ii_ // 4
                q_b = (0, 1, 3, 2)[ii_ % 4]
                prb = slice(32 * q_b, 32 * q_b + 32)
                b_r = nb()
                b_i = nb()
                if q_b < 3:
                    MM(b_r, b_r.ap[:, :N], Bblk[prb, kt_b, 0, :], xb[prb, kt_b, :N], [S5W, xbr[kt_b]], True, True)
                    MM(b_i, b_i.ap[:, :N], Bblk[prb, kt_b, 1, :], xb[prb, kt_b, :N], [S5W, xbr[kt_b]], True, True)
                else:
                    MM(b_r, b_r.ap[:, :N], Bq3[64:128, kt_b, 0, :], xb[64:128, kt_b, :N], [S5W, xbr[kt_b]], True, True)
                    MM(b_i, b_i.ap[:, :N], Bq3[64:128, kt_b, 1, :], xb[64:128, kt_b, :N], [S5W, xbr[kt_b]], True, True)
                return b_r, b_i

            nxtB = emitB(0)
            for ii in range(64):
                kt = ii // 4
                q = (0, 1, 3, 2)[ii % 4]
                i = kt * 4 + q
                pr = slice(32 * q, 32 * q + 32)
                Hr, Hi, RHk = HrA[ii % 2], HiA[ii % 2], RH[ii % 2]
                br_, bi_ = nxtB
                if not is_s:
                    B.op(DVE, lambda: nc.vector.tensor_scalar(out=kfi[:, :N], in0=iota1[:, :N], scalar1=thn[:, i:i + 1],
                                                              scalar2=None, op0=ALU.mult), [AR, CR], [Rt])
                    V_stt(kf[:, :N], iota1[:, :N], thn[:, i:i + 1], kfi[:, :N], ALU.mult, ALU.subtract, [AR, CR, Rt], [Rt])
                    A_f(sn[:, :N], kf[:, :N], AF.Sin, [Rt], [Rt], scale=SIN_SCALE)
                    A_f(kf[:, :N], kf[:, :N], AF.Abs, [Rt], [Rt])
                    A_f(cs[:, :N], kf[:, :N], AF.Sin, [Rt, CR], [Rt], scale=-SIN_SCALE, bias=cbias[:, 0:1])
                    V_tt(u1[:, :N], cs[:, :N], br_.ap[:, :N], ALU.mult, [Rt, br_.reg], [Ru])
                    V_tt(u2[:, :N], sn[:, :N], bi_.ap[:, :N], ALU.mult, [Rt, bi_.reg], [Ru])
                    V_tt(u1[:, :N], u1[:, :N], u2[:, :N], ALU.add, [Ru], [Ru])
                    V_tt(u2[:, :N], cs[:, :N], bi_.ap[:, :N], ALU.mult, [Rt, bi_.reg], [Ru])
                    V_tt(gr[:, :N], sn[:, :N], br_.ap[:, :N], ALU.mult, [Rt, br_.reg], [Ru])
                    V_tt(u2[:, :N], u2[:, :N], gr[:, :N], ALU.subtract, [Ru], [Ru])
                    rb = rho[:, i:i + 1].to_broadcast([128, N])
                    B.op(DVE, lambda: nc.vector.tensor_tensor_scan(out=gr[:, :N], data0=rb, data1=u1[:, :N],
                                                                   initial=hst[:, j, 0, i:i + 1], op0=ALU.mult, op1=ALU.add),
                         [AR, Ru, hstR], [Ru])
                    B.op(DVE, lambda: nc.vector.tensor_tensor_scan(out=gi[:, :N], data0=rb, data1=u2[:, :N],
                                                                   initial=hst[:, j, 1, i:i + 1], op0=ALU.mult, op1=ALU.add),
                         [AR, Ru, hstR], [Ru, Ry])
                    V_tt(u1[:, :N], cs[:, :N], gr[:, :N], ALU.mult, [Rt, Ru], [Ru])
                    V_tt(u2[:, :N], sn[:, :N], gi[:, :N], ALU.mult, [Rt, Ru], [Ru])
                    V_tt(Hr[:, :N], u1[:, :N], u2[:, :N], ALU.subtract, [Ru], [RHk])
                    V_stt(hst[:, j, 0, i:i + 1], u1[:, N - 1:N], 1.0, u2[:, N - 1:N], ALU.mult, ALU.subtract, [Ru, hstR], [hstR])
                    V_tt(u1[:, :N], cs[:, :N], gi[:, :N], ALU.mult, [Rt, Ru], [Ru])
                    V_tt(u2[:, :N], sn[:, :N], gr[:, :N], ALU.mult, [Rt, Ru], [Ru])
                    V_tt(Hi[:, :N], u1[:, :N], u2[:, :N], ALU.add, [Ru], [RHk])
                    V_stt(hst[:, j, 1, i:i + 1], u1[:, N - 1:N], 1.0, u2[:, N - 1:N], ALU.mult, ALU.add, [Ru, hstR], [hstR])
                else:
                    V_ts(u1[:, :N], h0[:, 0, i, :], abr[:, i:i + 1], ALU.mult, [AR, Ru], [Ru])
                    V_stt(u1[:, :N], h0[:, 1, i, :], nabi[:, i:i + 1], u1[:, :N], ALU.mult, ALU.add, [AR, Ru], [Ru])
                    V_tt(hn_[:, 0, i, :], u1[:, :N], br_.ap[:, :N], ALU.add, [Ru, br_.reg], [Ru])
                    V_ts(u2[:, :N], h0[:, 1, i, :], abr[:, i:i + 1], ALU.mult, [AR, Ru], [Ru])
                    V_stt(u2[:, :N], h0[:, 0, i, :], abi[:, i:i + 1], u2[:, :N], ALU.mult, ALU.add, [AR, Ru], [Ru])
                    V_tt(hn_[:, 1, i, :], u2[:, :N], bi_.ap[:, :N], ALU.add, [Ru, bi_.reg], [Ru])
                    V_cp(Hr[:, :N], hn_[:, 0, i, :], [Ru], [RHk])
                    V_cp(Hi[:, :N], hn_[:, 1, i, :], [Ru], [RHk])
                if ii + 1 < 64:
                    nxtB = emitB(ii + 1)
                if q == 0:
                    ybank = nb()
                    reserved.add(banks.index(ybank))
                if q == 3:
                    ylh0, ylh1, yout, yst = Cq3[:, kt, 0, :], Cq3[:, kt, 1, :], ybank.ap[64:128, :N], True
                else:
                    ylh0, ylh1, yout, yst = Cblk[:, i, 0, :], Cblk[:, i, 1, :], ybank.ap[pr, :N], q != 2
                B.op(PE, lambda: nc.tensor.matmul(yout, ylh0, Hr[:, :N], start=yst, stop=False), [S5W, RHk], [ybank.reg])
                B.op(PE, lambda: nc.tensor.matmul(yout, ylh1, Hi[:, :N], start=False, stop=True), [S5W, RHk], [ybank.reg])
                if ii % 4 == 3:
                    V_stt(ysb[:, :N], x32[:, kt, :N], s5d[:, j * KT + kt:j * KT + kt + 1], ybank.ap[:, :N],
                          ALU.mult, ALU.add, [x32r[kt], ybank.reg, CR, Ru], [Ru, Ry])
                    A_f(zb[:, kt, :N], ysb[:, :N], AF.Gelu_apprx_tanh, [Ry], [Rz])
                    reserved.discard(banks.index(ybank))
            if is_s:
                for r_ in range(2):
                    B.dma(SP, o_s_s5[j, r_], hn_[:, r_], [Ru], [], stsem)
            collect_arena([S5W, Rt, Ru, Ry, Rz] + RH)
            AR.w = Rz.w if Rz.w is not None else AR.w
            R = [AR]
            sg = av(P0, NP)
            pr2 = av(P0 + 2048, NP)

            def evac(dt, bs):
                A_f(sg[:, :N], bs[1].ap[:, :N], AF.Sigmoid, [bs[1].reg] + R, R)
                V_tt(pr2[:, :N], sg[:, :N], bs[0].ap[:, :N], ALU.mult, [bs[0].reg] + R, R)
                V_stt(x32[:, dt, :N], pr2[:, :N], 1.0 / ALPHA, x32[:, dt, :N], ALU.mult, ALU.add,
                      R + [x32r[dt]], [x32r[dt]])

            stream_fm([s5_wa[j], s5_wb[j]], D, N, lambda kt: (zb[:, kt, :N], AR), evac)
            layer_norm(l, 1, N)

        def log_sigmoid(out, z, tmp, npart, R):
            A_f(tmp, z, AF.Abs, R, R)
            A_f(tmp, tmp, AF.Exp, R, R, scale=-1.0)
            A_f(tmp, tmp, AF.Ln, R + [CR], R, bias=cbias[0:npart, 3:4])
            V_ts(out, z, 0.0, ALU.min, R, R)
            V_tt(out, out, tmp, ALU.subtract, R, R)

        def head_norm(hs, np_, g_ap, out_ap, st6, mv, R):
            B.op(DVE, lambda: nc.vector.bn_stats(out=st6, in_=hs), R, R)
            B.op(DVE, lambda: nc.vector.bn_aggr(out=mv, in_=st6), R, R)
            A_f(mv[:, 1:2], mv[:, 1:2], AF.Sqrt, R + [CR], R, bias=cbias[0:np_, 2:3])
            B.op(DVE, lambda: nc.vector.reciprocal(out=mv[:, 1:2], in_=mv[:, 1:2]), R, R)
            V_ts(hs, hs, mv[:, 0:1], ALU.subtract, R, R, s2=mv[:, 1:2], op1=ALU.mult)
            V_tt(out_ap, hs, g_ap, ALU.mult, R, R)

        def ml_out_and_norm(l, j, N, hnT, sg):
            def evac_o(ct, bs):
                A_f(sg[:, :N], bs[0].ap[:, :N], AF.Sigmoid, [bs[0].reg, AR], [AR])
                V_tt(hnT[:, ct, :N], hnT[:, ct, :N], sg[:, :N], ALU.mult, [AR], [AR])

            stream_fm([ml_win[j][:, 4096:6144]], D, N, src_xb(N), evac_o)

            def evac_y(dt, bs):
                V_stt(x32[:, dt, :N], bs[0].ap[:, :N], 1.0 / ALPHA, x32[:, dt, :N], ALU.mult, ALU.add,
                      [bs[0].reg, x32r[dt]], [x32r[dt]])

            stream_fm([ml_wout[j]], D, N, lambda kt: (hnT[:, kt, :N], AR), evac_y)
            layer_norm(l, 1, N)

        def load_gate_w(j, off):
            wg = av(off, KT * 8, BF16).rearrange("p (k c) -> p k c", k=KT)
            wgf = av(off + 256, KT * 8).rearrange("p (k c) -> p k c", k=KT)
            with nc.allow_non_contiguous_dma(reason="tiny gate weights"):
                B.dma(SP, wgf, ml_win[j][:, 6144:6152].rearrange("(k p) c -> p k c", p=128), [], [AR], ldsem)
            V_cp(wg, wgf, [AR], [AR])
            return wg

        def mlstm_p(l, j, N, first_tile):
            R = [AR]
            NCH = N // 128
            qT = av(0, 2 * NP, BF16).rearrange("p (k n) -> p k n", k=2)
            kT = av(2048, 2 * NP, BF16).rearrange("p (k n) -> p k n", k=2)
            v_tm = av(4096, 4 * DV, BF16).rearrange("p (c f) -> p c f", c=4)
            C32 = av(8192, 2 * DV).rearrange("p (k v) -> p k v", k=2)
            Cb = av(12288, 2 * DV, BF16).rearrange("p (k v) -> p k v", k=2)
            gbc = av(14336, DV)
            wg = load_gate_w(j, 16384)
            S0 = 17408
            ET = av(S0, 128)
            scT = av(S0 + 512, 128, BF16)
            qs = av(S0 + 768, 256, BF16).rearrange("p (k n) -> p k n", k=2)
            kd = av(S0 + 1280, 256, BF16)
            hs = av(S0 + 1792, DV)
            hno = av(S0 + 3840, DV, BF16)
            cols = av(S0 + 4864, 4)
            bcs = av(S0 + 4880, 132)
            nbf = av(S0 + 5408, 2, BF16)
            st6 = av(S0 + 5412, 6)
            mv = av(S0 + 5436, 2)
            dsc = av(S0 + 5444, 1)
            sg = av(24576, NP)
            hnT = av(32768, KT * NP, BF16).rearrange("p (k n) -> p k n", k=KT)
            ig = rw[:, 0:512]
            lf = rw[:, 512:1024]
            ltmp = rw[:, 1024:1536]
            bcum = rw[:, 1024:1152]
            a_ = rw[:, 1152:1280]
            M_ = rw[:, 1280:1408]
            mt_ = rw[:, 1408:1536]
            dec_ = rw[:, 1536:1664]
            Rb = rw[:, 1664:1921]
            ones1 = consts[0:1, C_ONES:C_ONES + 128]
            one11 = consts[0:1, C_ONES:C_ONES + 1]
            for h in range(NH):
                jh = j * 4 + h
                bgi = nb()
                bgf = nb()
                for kt in range(KT):
                    MM(bgi, bgi.ap[0:1, :N], wg[:, kt, h:h + 1], xb[:, kt, :N], [AR, xbr[kt]], kt == 0, kt == KT - 1)
                for kt in range(KT):
                    MM(bgf, bgf.ap[0:1, :N], wg[:, kt, 4 + h:5 + h], xb[:, kt, :N], [AR, xbr[kt]], kt == 0, kt == KT - 1)
                A_f(ig[:, :N], bgi.ap[0:1, :N], AF.Identity, [bgi.reg, CR, RWr], [RWr], bias=mlb0[:, jh * 2:jh * 2 + 1])
                A_f(lf[:, :N], bgf.ap[0:1, :N], AF.Identity, [bgf.reg, CR, RWr], [RWr], bias=mlb0[:, jh * 2 + 1:jh * 2 + 2])
                log_sigmoid(lf[:, :N], lf[:, :N], ltmp[:, :N], 1, [RWr])

                def evac_q(ct, bs):
                    A_f(qT[:, ct, :N], bs[0].ap[:, :N], AF.Copy, [bs[0].reg] + R, R)

                def evac_k(ct, bs):
                    A_f(kT[:, ct, :N], bs[0].ap[:, :N], AF.Copy, [bs[0].reg] + R, R, scale=DK ** -0.5)

                stream_fm([ml_win[j][:, h * DK:(h + 1) * DK]], DK, N, src_xb(N), evac_q)
                stream_fm([ml_win[j][:, 1024 + h * DK:1024 + (h + 1) * DK]], DK, N, src_xb(N), evac_k)

                def evac_v(cc, ti, bk):
                    A_f(v_tm[:, ti, :], bk.ap[:, :], AF.Copy, [bk.reg] + R, R)

                stream_tm(ml_win[j], 2048 + h * DV, 1, [(c * 128, 128) for c in range(NCH)], evac_v)
                B.dma(SP, gbc, ml_gbc[j][:, h * DV:(h + 1) * DV], [], R, ldsem)
                if first_tile:
                    B.op(DVE, lambda: nc.vector.memset(C32, 0.0), R, R)
                else:
                    B.dma(SP, C32, o_p_C[j, h].rearrange("(k p) v -> p k v", p=128), [PCr[jh]], R, ldsem)
                A_f(Cb, C32, AF.Copy, R, R)
                for c in range(NCH):
                    tsl = slice(c * 128, (c + 1) * 128)
                    RR = [RWr]
                    B.op(DVE, lambda: nc.vector.tensor_tensor_scan(out=bcum, data0=ones1, data1=lf[:, tsl], initial=0.0,
                                                                   op0=ALU.mult, op1=ALU.add), RR + [CR], RR)
                    V_tt(a_, ig[:, tsl], bcum, ALU.subtract, RR, RR)
                    B.op(DVE, lambda: nc.vector.tensor_tensor_scan(out=M_, data0=ones1, data1=a_, initial=mrow[:, jh:jh + 1],
                                                                   op0=ALU.mult, op1=ALU.max), RR + [CR, mR], RR)
                    V_ts(Rb[:, 0:128], M_, -1.0, ALU.mult, RR, RR)
                    A_f(Rb[:, 128:256], Rb[:, 0:128], AF.Exp, RR + [mR], RR, bias=mrow[:, jh:jh + 1])
                    V_cp(Rb[:, 256:257], Rb[:, 255:256], RR, RR)
                    V_tt(mt_, bcum, M_, ALU.add, RR, RR)
                    V_cp(mrow[:, jh:jh + 1], mt_[:, 127:128], RR + [mR], [mR])
                    A_f(mt_, mt_, AF.Exp, RR, RR, scale=-1.0)
                    A_f(dec_, a_, AF.Exp, RR, RR, bias=Rb[:, 127:128])
                    bc_ = nb()
                    for ci, row in enumerate((a_, dec_, mt_)):
                        B.op(PE, lambda: nc.tensor.matmul(bc_.ap[:, ci:ci + 1], row, one11, start=True, stop=True),
                             RR + [CR], [bc_.reg])
                    V_cp(cols[:, 0:3], bc_.ap[:, 0:3], [bc_.reg] + R, R)
                    bb = nb()
                    B.op(PE, lambda: nc.tensor.matmul(bb.ap[:, 0:257], ones1, Rb, start=True, stop=True), RR + [CR], [bb.reg])
                    V_cp(bcs[:, 0:129], bb.ap[:, 128:257], [bb.reg] + R, R)
                    bs_ = nb()
                    for dk in range(2):
                        MM(bs_, bs_.ap[:, 0:128], kT[:, dk, tsl], qT[:, dk, tsl], R, dk == 0, dk == 1)
                    A_f(ET, bb.ap[:, 0:128], AF.Exp, [bb.reg] + R, R, bias=cols[:, 0:1])
                    V_tt(ET, ET, maskT, ALU.mult, R + [CR], R)
                    V_tt(scT, bs_.ap[:, 0:128], ET, ALU.mult, [bs_.reg] + R, R)
                    for dk in range(2):
                        V_tt(qs[:, dk, :], qT[:, dk, tsl], bcs[:, 0:128], ALU.mult, R, R)
                    bn_ = nb()
                    MM(bn_, bn_.ap[:, :], scT, v_tm[:, c, :], R, True, False)
                    for dk in range(2):
                        MM(bn_, bn_.ap[:, :], qs[:, dk, :], Cb[:, dk, :], R, False, dk == 1)
                    V_cp(nbf, n32[:, j, h * 2:h * 2 + 2], R + [nR], R)
                    bd_ = nb()
                    MM(bd_, bd_.ap[:, 0:1], scT, onesb[:, 0:1], R + [CR], True, False)
                    for dk in range(2):
                        MM(bd_, bd_.ap[:, 0:1], qs[:, dk, :], nbf[:, dk:dk + 1], R, False, dk == 1)
                    A_f(dsc, bd_.ap[:, 0:1], AF.Abs, [bd_.reg] + R, R)
                    V_tt(dsc, dsc, cols[:, 2:3], ALU.max, R, R)
                    B.op(DVE, lambda: nc.vector.reciprocal(out=dsc, in_=dsc), R, R)
                    A_f(hs, bn_.ap[:, :], AF.Copy, [bn_.reg] + R, R, scale=dsc[:, 0:1])
                    head_norm(hs, 128, gbc, hno, st6, mv, R)
                    bt = nb()
                    btb = bt.ap.bitcast(BF16)
                    for jj in range(4):
                        TR(bt, btb[:, jj * 128:(jj + 1) * 128], hno[:, jj * 128:(jj + 1) * 128], identb, R + [CR])
                    A_f(hnT[:, h * 4:(h + 1) * 4, tsl], btb[:, 0:512].rearrange("p (a t) -> p a t", a=4), AF.Copy,
                        [bt.reg] + R, R)
                    bk_ = nb()
                    bkb = bk_.ap.bitcast(BF16)
                    for dk in range(2):
                        TR(bk_, bkb[:, dk * 128:(dk + 1) * 128], kT[:, dk, tsl], identb, R + [CR])
                    V_ts(kd, bkb[:, 0:256], cols[:, 1:2], ALU.mult, [bk_.reg] + R, R)
                    for dk in range(2):
                        bC = nb()
                        MM(bC, bC.ap[:, :], kd[:, dk * 128:(dk + 1) * 128], v_tm[:, c, :], R, True, True)
                        V_stt(C32[:, dk, :], C32[:, dk, :], bcs[:, 128:129], bC.ap[:, :], ALU.mult, ALU.add,
                              [bC.reg] + R, R)
                    A_f(Cb, C32, AF.Copy, R, R)
                    bN = nb()
                    for dk in range(2):
                        B.op(PE, lambda: nc.tensor.matmul(bN.ap[:, dk:dk + 1], kd[:, dk * 128:(dk + 1) * 128], onesb[:, 0:1],
                                                          start=True, stop=True), R + [CR], [bN.reg])
                    V_stt(n32[:, j, h * 2:h * 2 + 2], n32[:, j, h * 2:h * 2 + 2], bcs[:, 128:129], bN.ap[:, 0:2],
                          ALU.mult, ALU.add, [bN.reg, nR] + R, [nR])
                B.dma(SP, o_p_C[j, h].rearrange("(k p) v -> p k v", p=128), C32, R, [PCr[jh]], stsem)
            ml_out_and_norm(l, j, N, hnT, sg)

        def mlstm_s(l, j):
            R = [AR]
            N = NS
            q_tm = av(0, 1024)[0:16]
            k_tm = av(4096, 1024)[0:16]
            v_tm = av(8192, 2048)[0:16]
            n0 = av(16384, 1024)[0:16]
            kw = av(20480, 1024)[0:16]
            kz = av(24576, 1024)[0:16]
            qT32 = av(28672, 128).rearrange("p (k n) -> p k n", k=8)
            qz = av(29184, 8 * 256).rearrange("p (k b c) -> p k b c", k=8, b=16)
            hnT = av(37376, KT * NS, BF16).rearrange("p (k n) -> p k n", k=KT)
            hno = av(37888, D, BF16)[0:16]
            tmpA = av(41984, 1024)[0:16]
            hs = av(46080, DV)[0:16]
            gbc = av(48128, DV)[0:16]
            SM = 50176
            def sm(k, n=4):
                return av(SM + 16 * k, n)[0:16]
            ig, zf, lf, m0, pp, mt, wts, scl, emt, qk, qn, sc, den, rr, tt4 = [sm(k) for k in range(15)]
            g8 = av(SM + 256, 8)[0:16]
            bif = av(SM + 288, 8)[0:16]
            D_ = av(SM + 320, 64)[0:16]
            scbc = av(SM + 576, 64)
            st6 = av(SM + 832, 6)[0:16]
            mv = av(SM + 856, 2)[0:16]
            wg = load_gate_w(j, SM + 1024)
            sg = av(SM + 2048 - 256 + 768, NS)
            Cbuf = [av(53248 + 4096 * k, 2 * DV).rearrange("p (k v) -> p k v", k=2) for k in range(2)]
            Cbr = [Reg("cbuf0"), Reg("cbuf1")]
            for r_ in Cbr:
                r_.w = AR.w
                r_.rs = dict(AR.rs)
            bg = nb()
            for kt in range(KT):
                MM(bg, bg.ap[0:16, 0:8], xb[:, kt, 0:16], wg[:, kt, :], [AR, xbr[kt]], kt == 0, kt == KT - 1)
            B.dma(SP, bif, ml_bif_tm[j], [], R, ldsem)
            B.dma(SP, m0, st_m[j], [], R, ldsem)
            B.dma(SP, n0, st_n[j], [], R, ldsem)
            V_tt(g8, bg.ap[0:16, 0:8], bif, ALU.add, [bg.reg] + R, R)
            V_cp(ig, g8[:, 0:4], R, R)
            V_cp(zf, g8[:, 4:8], R, R)
            log_sigmoid(lf, zf, tt4, 16, R)
            V_tt(pp, lf, m0, ALU.add, R, R)
            V_tt(mt, pp, ig, ALU.max, R, R)
            V_tt(wts, ig, mt, ALU.subtract, R, R)
            A_f(wts, wts, AF.Exp, R, R)
            V_tt(scl, pp, mt, ALU.subtract, R, R)
            A_f(scl, scl, AF.Exp, R, R)
            A_f(emt, mt, AF.Exp, R, R, scale=-1.0)
            B.dma(SP, o_s_m[j], mt, R, [], stsem)

            def evac_qT(ct, bs):
                A_f(qT32[:, ct, :], bs[0].ap[:, :N], AF.Copy, [bs[0].reg] + R, R)

            stream_fm([ml_win[j][:, 0:1024]], 1024, N, src_xb(N), evac_qT)

            def evac_tm(dst, scale):
                def f(cc, ti, bk):
                    A_f(dst[:, cc * 512:(cc + 1) * 512], bk.ap[0:16, :], AF.Copy, [bk.reg] + R, R, scale=scale)
                return f

            stream_tm(ml_win[j], 0, 2, [(0, 16)], evac_tm(q_tm, 1.0))
            stream_tm(ml_win[j], 1024, 2, [(0, 16)], evac_tm(k_tm, DK ** -0.5))
            stream_tm(ml_win[j], 2048, 4, [(0, 16)], evac_tm(v_tm, 1.0))
            V_tt(tmpA, q_tm, k_tm, ALU.mult, R, R)
            B.op(DVE, lambda: nc.vector.tensor_reduce(out=qk, in_=tmpA.rearrange("p (h d) -> p h d", h=4), axis=AX.X,
                                                      op=ALU.add), R, R)
            V_tt(tmpA, q_tm, n0, ALU.mult, R, R)
            B.op(DVE, lambda: nc.vector.tensor_reduce(out=qn, in_=tmpA.rearrange("p (h d) -> p h d", h=4), axis=AX.X,
                                                      op=ALU.add), R, R)
            V_tt(sc, qk, wts, ALU.mult, R, R)
            V_tt(den, scl, qn, ALU.mult, R, R)
            V_tt(den, den, sc, ALU.add, R, R)
            A_f(rr, den, AF.Abs, R, R)
            V_tt(rr, rr, emt, ALU.max, R, R)
            B.op(DVE, lambda: nc.vector.reciprocal(out=rr, in_=rr), R, R)
            k3 = k_tm.rearrange("p (h d) -> p h d", h=4)
            kw3 = kw.rearrange("p (h d) -> p h d", h=4)
            V_tt(kw3, k3, wts.unsqueeze(2).to_broadcast([16, 4, DK]), ALU.mult, R, R)
            t3 = tmpA.rearrange("p (h d) -> p h d", h=4)
            V_tt(t3, n0.rearrange("p (h d) -> p h d", h=4), scl.unsqueeze(2).to_broadcast([16, 4, DK]), ALU.mult, R, R)
            V_tt(tmpA, tmpA, kw, ALU.add, R, R)
            B.dma(SP, o_s_n[j], tmpA, R, [], stsem)
            B.op(DVE, lambda: nc.vector.memset(qz, 0.0), R, R)
            for b in range(NS):
                V_cp(qz[:, :, b, b:b + 1], qT32[:, :, b:b + 1], R, R)
            V_tt(D_.rearrange("p (b h) -> p b h", b=16), I16.unsqueeze(2).to_broadcast([16, 16, 4]),
                 scl.unsqueeze(1).to_broadcast([16, 16, 4]), ALU.mult, R + [CR], R)
            bsb = nb()
            B.op(PE, lambda: nc.tensor.matmul(bsb.ap[:, 0:64], consts[0:16, C_ONES:C_ONES + 128], D_, start=True, stop=True),
                 R + [CR], [bsb.reg])
            V_cp(scbc, bsb.ap[:, 0:64], [bsb.reg] + R, R)
            qbanks = []
            for h in range(NH):
                bk = nb()
                reserved.add(banks.index(bk))
                qbanks.append(bk)
            cnt = 0
            for b in range(NS):
                V_ts(kz, kw, I16[:, b:b + 1], ALU.mult, R + [CR], R)
                for h in range(NH):
                    cbuf = Cbuf[cnt % 2]
                    cr = Cbr[cnt % 2]
                    cnt += 1
                    B.dma(GQ, cbuf, st_C[j, b, h].rearrange("(k p) v -> p k v", p=128), [], [cr], csem)
                    for dk in range(2):
                        B.op(PE, lambda: nc.tensor.matmul(qbanks[h].ap[0:16, :], qz[:, h * 2 + dk, b, :], cbuf[:, dk, :],
                                                          start=(b == 0 and dk == 0), stop=(b == NS - 1 and dk == 1)),
                             [cr, AR], [qbanks[h].reg], sig=True)
                    for dk in range(2):
                        bo = nb()
                        B.op(PE, lambda: nc.tensor.matmul(bo.ap[:, :], kz[:, h * DK + dk * 128:h * DK + (dk + 1) * 128],
                                                          v_tm[:, h * DV:(h + 1) * DV], start=True, stop=True),
                             [AR], [bo.reg])
                        V_stt(cbuf[:, dk, :], cbuf[:, dk, :], scbc[:, b * 4 + h:b * 4 + h + 1], bo.ap[:, :], ALU.mult, ALU.add,
                              [bo.reg, cr, AR], [cr])
                    B.dma(SP, o_s_C[j, b, h].rearrange("(k p) v -> p k v", p=128), cbuf, [cr], [], stsem)
            for h in range(NH):
                B.dma(SP, gbc, ml_gbc[j][0:16, h * DV:(h + 1) * DV], [], R, ldsem)
                V_ts(hs, v_tm[:, h * DV:(h + 1) * DV], sc[:, h:h + 1], ALU.mult, R, R)
                V_stt(hs, qbanks[h].ap[0:16, :], scl[:, h:h + 1], hs, ALU.mult, ALU.add, [qbanks[h].reg] + R, R)
                V_ts(hs, hs, rr[:, h:h + 1], ALU.mult, R, R)
                head_norm(hs, 16, gbc, hno[:, h * DV:(h + 1) * DV], st6, mv, R)
                reserved.discard(banks.index(qbanks[h]))
            bt = nb()
            btb = bt.ap.bitcast(BF16)
            for kt in range(KT):
                TR(bt, btb[:, kt * 16:(kt + 1) * 16], hno[:, kt * 128:(kt + 1) * 128], identb[0:16, 0:16], R + [CR])
            A_f(hnT, btb[:, 0:KT * 16].rearrange("p (k n) -> p k n", k=KT), AF.Copy, [bt.reg] + R, R)
            collect_arena(Cbr)
            ml_out_and_norm(l, j, N, hnT, sg)

        tiles = [("p", t) for t in range(SEQ // NP)] + [("s", 0)]
        if dbg >= 300:
            tiles = tiles[:dbg - 300] + tiles[-1:]
            dbg = 0
        elif dbg >= 200:
            tiles = tiles[dbg - 200:]
            dbg = 0
        elif dbg >= 100:
            tiles = tiles[:dbg - 100]
            dbg = 0
        elif dbg:
            tiles = [("p", 0)]
        for kind, t in tiles:
            is_s = kind == "s"
            N = NS if is_s else NP
            wmode[0] = "store" if (tiles[0] == (kind, t)) and not is_s else ("load" if tiles[0][0] == "p" else "none")
            if is_s:
                B.dma(SP, x32[:, :, :N], xT_s.rearrange("(k p) t -> p k t", p=128), [], x32r, ldsem)
            else:
                B.dma(SP, x32[:, :, :N], xT_p[:, t * NP:(t + 1) * NP].rearrange("(k p) t -> p k t", p=128), [], x32r, ldsem)
            A_f(xb[:, :, :N], x32[:, :, :N], AF.Copy, x32r, xbr)
            stage = 0
            for l in range(DEPTH):
                j = l // 2
                ffn(l, 0, N)
                stage += 1
                if dbg and stage >= dbg:
                    break
                if l % 2 == 0:
                    s5_mix(l, j, N, is_s)
                elif is_s:
                    mlstm_s(l, j)
                else:
                    mlstm_p(l, j, N, t == 0)
                stage += 1
                if dbg and stage >= dbg:
                    break
                ffn(l, 1, N)
                stage += 1
                if dbg and stage >= dbg:
                    break
            if is_s:
                B.dma(SP, yT_s.rearrange("(k p) t -> p k t", p=128), x32[:, :, :N], x32r, [], stsem)
            else:
                B.dma(SP, yT_p[:, t * NP:(t + 1) * NP].rearrange("(k p) t -> p k t", p=128), x32[:, :, :N], x32r, [], stsem)
            if kind == "p" and t == SEQ // NP - 1:
                for j in range(2):
                    for r_ in range(2):
                        B.dma(SP, o_p_s5[j, r_], hst[:, j, r_, :], [hstR], [], stsem)
                    B.dma(SP, o_p_n[j], n32[:, j, :], [nR], [], stsem)
                B.dma(SP, o_p_m.rearrange("j h o -> o (j h)"), mrow[:, :], [mR], [], stsem)
        nc.sync.wait_ge(stsem.h, stsem.n)
        if scsem.n:
            nc.sync.wait_ge(scsem.h, scsem.n)
        for e in (PE, ACT, DVE):
            nc.sync.wait_ge(e.sem.h, e.sem.n)
    return nc


_CACHE = {}


def _consts():
    c = np.zeros((128, CW), np.float32)
    c[:, C_ID:C_ID + 128] = np.eye(128, dtype=np.float32)
    s = np.arange(128)[:, None]
    t = np.arange(128)[None, :]
    c[:, C_MASKT:C_MASKT + 128] = (s <= t).astype(np.float32)
    c[:, C_IOTA:C_IOTA + 512] = np.arange(1, 513, dtype=np.float32)[None, :]
    for h in range(4):
        c[h, C_SEL + h * 128:C_SEL + (h + 1) * 128] = 1.0
    c[0:16, C_I16:C_I16 + 16] = np.eye(16, dtype=np.float32)
    p = np.arange(128)
    for g2 in range(2):
        c[:, C_MG2 + g2] = ((p // 16) % 2 == g2).astype(np.float32)
        c[:, C_MH + g2] = ((p // 64) == g2).astype(np.float32)
    c[:, C_ONES:C_ONES + 128] = 1.0
    c[:, C_M96] = (p >= 96).astype(np.float32)
    return c


def kernel(x_prompt, x_sample, state_s5_re, state_s5_im, state_mlstm_C, state_mlstm_n, state_mlstm_m,
           ln_g, ln_b, ffn_w_gate, ffn_w_up, ffn_w_down,
           s5_a_re, s5_a_im, s5_log_dt, s5_b_re, s5_b_im, s5_c_re, s5_c_im, s5_d, s5_w_a, s5_w_b,
           ml_w_in, ml_b_i, ml_b_f, ml_norm_g, ml_w_out):
    f = lambda a: np.ascontiguousarray(np.asarray(a, dtype=np.float32))
    x_prompt, x_sample = f(x_prompt), f(x_sample)
    if "nc" not in _CACHE:
        _CACHE["nc"] = build_program()
    nc = _CACHE["nc"]

    def fm(v):
        v = f(v).reshape(-1, KT, 128)
        return np.ascontiguousarray(v.transpose(2, 0, 1).reshape(128, -1))

    def e_lay(a):
        a = np.repeat(f(a), 16, axis=1)
        return np.ascontiguousarray(a.reshape(2, KT, 128, -1).transpose(0, 2, 1, 3))

    def h_lay(a):
        return np.ascontiguousarray(f(a).reshape(2, 64, 2, PS).transpose(0, 2, 3, 1).reshape(2, 128, 64))

    shared = {
        "lng": fm(ln_g), "lnb": fm(ln_b),
        "w_gate": f(ffn_w_gate), "w_up": f(ffn_w_up), "w_down": f(ffn_w_down),
        "aE_re": e_lay(s5_a_re), "aE_im": e_lay(s5_a_im),
        "ldtE": np.ascontiguousarray(np.repeat(f(s5_log_dt), 16, axis=1).reshape(2, KT, 128).transpose(0, 2, 1)),
        "bE_re": np.ascontiguousarray(f(s5_b_re).transpose(0, 1, 3, 2).reshape(2, KT, 128, PS).transpose(0, 2, 1, 3)),
        "bE_im": np.ascontiguousarray(f(s5_b_im).transpose(0, 1, 3, 2).reshape(2, KT, 128, PS).transpose(0, 2, 1, 3)),
        "aH_re": h_lay(s5_a_re), "aH_im": h_lay(s5_a_im),
        "ldtH": np.ascontiguousarray(np.repeat(f(s5_log_dt).reshape(2, 64, 2, 1), PS, axis=3).transpose(0, 2, 3, 1).reshape(2, 128, 64)),
        "cH_re": np.ascontiguousarray(f(s5_c_re).reshape(2, 64, 2, 16, PS).transpose(0, 2, 4, 1, 3).reshape(2, 128, 64, 16)),
        "cH_im": np.ascontiguousarray(f(s5_c_im).reshape(2, 64, 2, 16, PS).transpose(0, 2, 4, 1, 3).reshape(2, 128, 64, 16)),
        "s5d": fm(s5_d), "s5_wa": f(s5_w_a), "s5_wb": f(s5_w_b),
        "ml_win": f(ml_w_in), "ml_wout": f(ml_w_out),
        "ml_bif0": np.ascontiguousarray(np.stack([f(ml_b_i), f(ml_b_f)], axis=-1).reshape(1, 16)),
        "ml_bif_tm": np.ascontiguousarray(np.broadcast_to(
            np.concatenate([f(ml_b_i), f(ml_b_f)], axis=1)[:, None, :], (2, 16, 8))),
        "ml_gbc": np.ascontiguousarray(np.broadcast_to(f(ml_norm_g)[:, None, :], (2, 128, D))),
        "consts": _consts(),
    }
    s5re, s5im = f(state_s5_re), f(state_s5_im)
    stC, stn, stm = f(state_mlstm_C), f(state_mlstm_n), f(state_mlstm_m)
    in_maps = []
    for c in range(N_CORES):
        bs = slice(NS * c, NS * (c + 1))
        m = dict(shared)
        m["xT_p"] = np.ascontiguousarray(x_prompt[c % 4].T)
        m["xT_s"] = np.ascontiguousarray(x_sample[bs, 0, :].T)
        def sl(a):
            return a[:, bs].reshape(2, NS, 64, 2, PS).transpose(0, 3, 4, 2, 1).reshape(2, 128, 64, NS)
        m["st_s5"] = np.ascontiguousarray(np.stack([sl(s5re), sl(s5im)], axis=1))
        m["st_C"] = np.ascontiguousarray(stC[:, bs])
        m["st_n"] = np.ascontiguousarray(stn[:, bs].reshape(2, NS, NH * DK))
        m["st_m"] = np.ascontiguousarray(stm[:, bs])
        in_maps.append(m)
    if _CACHE.get("prep_only"):
        return in_maps
    res = run_bass_kernel_spmd(nc, in_maps, core_ids=list(range(N_CORES)))
    rs = res.results
    y_prompt = np.stack([rs[b]["yT_p"].T for b in range(4)]).astype(np.float32)
    y_sample = np.concatenate([rs[c]["yT_s"].T[:, None, :] for c in range(N_CORES)], axis=0).astype(np.float32)

    def unh(a):
        lead = a.shape[:-2]
        return a.reshape(lead + (2, PS, 64)).transpose(tuple(range(len(lead))) + (len(lead) + 2, len(lead), len(lead) + 1)).reshape(lead + (G, PS))

    p_s5 = np.stack([rs[b]["o_p_s5"] for b in range(4)], axis=2)
    p_s5_re = np.ascontiguousarray(unh(p_s5[:, 0]))
    p_s5_im = np.ascontiguousarray(unh(p_s5[:, 1]))
    p_C = np.ascontiguousarray(np.stack([rs[b]["o_p_C"] for b in range(4)], axis=1))
    p_n = np.ascontiguousarray(np.stack(
        [rs[b]["o_p_n"].reshape(2, 128, NH, 2).transpose(0, 2, 3, 1).reshape(2, NH, DK) for b in range(4)], axis=1))
    p_m = np.ascontiguousarray(np.stack([rs[b]["o_p_m"].reshape(2, NH) for b in range(4)], axis=1))
    s_s5 = np.concatenate([rs[c]["o_s_s5"] for c in range(N_CORES)], axis=4)
    s_s5 = s_s5.reshape(2, 2, 2, PS, 64, NS * N_CORES).transpose(0, 1, 5, 4, 2, 3).reshape(2, 2, NS * N_CORES, G, PS)
    s_s5_re = np.ascontiguousarray(s_s5[:, 0])
    s_s5_im = np.ascontiguousarray(s_s5[:, 1])
    s_C = np.ascontiguousarray(np.concatenate([rs[c]["o_s_C"] for c in range(N_CORES)], axis=1))
    s_n = np.ascontiguousarray(np.concatenate([rs[c]["o_s_n"] for c in range(N_CORES)], axis=1).reshape(2, NS * N_CORES, NH, DK))
    s_m = np.ascontiguousarray(np.concatenate([rs[c]["o_s_m"] for c in range(N_CORES)], axis=1))
    return (y_prompt, y_sample, p_s5_re, p_s5_im, p_C, p_n, p_m, s_s5_re, s_s5_im, s_C, s_n, s_m)
```

```python
import contextlib
import math
import numpy as np
import concourse.bass as bass
import concourse.mybir as mybir
from concourse.bass_utils import run_bass_kernel_spmd

F32 = mybir.dt.float32
BF16 = mybir.dt.bfloat16
I32 = mybir.dt.int32
AF = mybir.ActivationFunctionType
ALU = mybir.AluOpType
AX = mybir.AxisListType

D = 2048
KT = 16
FF = 5504
MT = 43
SEQ = 2048
NP = 512
NS = 16
DEPTH = 4
NH = 4
DK = 256
DV = 512
G = 128
PS = 64
ALPHA = (2.0 * DEPTH) ** 0.25
EPS = 1e-5
EPS_LN = EPS / (ALPHA * ALPHA)
TWO_PI = 2.0 * math.pi
SIN_SCALE = TWO_PI * (1.0 - 2e-7)
N_CORES = 8

C_ID = 0
C_MASKT = 128
C_IOTA = 256
C_SEL = 768
C_I16 = 1280
C_MG2 = 1296
C_MH = 1298
C_ONES = 1300
C_M96 = 1428
CW = 1429


class Sem:
    def __init__(self, h):
        self.h = h
        self.n = 0


class Reg:
    __slots__ = ("name", "w", "rs")

    def __init__(self, name):
        self.name = name
        self.w = None
        self.rs = {}


class Eng:
    def __init__(self, h, sem, is_pe=False):
        self.h = h
        self.sem = sem
        self.seen = {}
        self.is_pe = is_pe
        self.pending = []


class Bank:
    def __init__(self, ap, reg):
        self.ap = ap
        self.reg = reg


class Builder:
    def __init__(self, nc, es):
        self.nc = nc
        self.es = es
        self.nsem = 0

    def sem(self, name):
        self.nsem += 1
        return Sem(self.es.enter_context(self.nc.semaphore(name)))

    def wait(self, eng, reads, writes):
        need = {}
        for r in reads:
            if r.w is not None:
                s, v = r.w
                if need.get(s, 0) < v:
                    need[s] = v
        for w in writes:
            if w.w is not None:
                s, v = w.w
                if need.get(s, 0) < v:
                    need[s] = v
            for s, v in w.rs.items():
                if need.get(s, 0) < v:
                    need[s] = v
        for s, v in need.items():
            if eng.is_pe and s is eng.sem:
                continue
            if eng.seen.get(s, 0) >= v:
                continue
            eng.h.wait_ge(s.h, v)
            eng.seen[s] = v

    def op(self, eng, fn, reads=(), writes=(), sig=True):
        self.wait(eng, reads, writes)
        ins = fn()
        if sig:
            eng.sem.n += 1
            ins.then_inc(eng.sem.h, 1)
            v = eng.sem.n
            for r in reads:
                r.rs[eng.sem] = v
            for r in eng.pending:
                r.rs[eng.sem] = v
            eng.pending = []
            for w in writes:
                w.w = (eng.sem, v)
                w.rs = {}
        else:
            eng.pending.extend(reads)
        return ins

    def dma(self, q, out, in_, reads, writes, dsem, **kw):
        self.wait(q, reads, writes)
        ins = q.h.dma_start(out=out, in_=in_, **kw)
        dsem.n += 16
        ins.then_inc(dsem.h, 16)
        v = dsem.n
        for r in reads:
            r.rs[dsem] = v
        for w in writes:
            w.w = (dsem, v)
            w.rs = {}


def build_program(dbg=0):
    nc = bass.Bass("TRN2", target_bir_lowering=False)

    def din(name, shape):
        return nc.dram_tensor(name, list(shape), F32, kind="ExternalInput").ap()

    def dout(name, shape):
        return nc.dram_tensor(name, list(shape), F32, kind="ExternalOutput").ap()

    xT_p = din("xT_p", [D, SEQ])
    xT_s = din("xT_s", [D, NS])
    d_lng = din("lng", [128, 192])
    d_lnb = din("lnb", [128, 192])
    w_gate = din("w_gate", [DEPTH, 2, D, FF])
    w_up = din("w_up", [DEPTH, 2, D, FF])
    w_down = din("w_down", [DEPTH, 2, FF, D])
    aE_re = din("aE_re", [2, 128, KT, PS])
    aE_im = din("aE_im", [2, 128, KT, PS])
    ldtE = din("ldtE", [2, 128, KT])
    bE_re = din("bE_re", [2, 128, KT, PS])
    bE_im = din("bE_im", [2, 128, KT, PS])
    aH_re = din("aH_re", [2, 128, 64])
    aH_im = din("aH_im", [2, 128, 64])
    ldtH = din("ldtH", [2, 128, 64])
    cH_re = din("cH_re", [2, 128, 64, 16])
    cH_im = din("cH_im", [2, 128, 64, 16])
    d_s5d = din("s5d", [128, 32])
    s5_wa = din("s5_wa", [2, D, D])
    s5_wb = din("s5_wb", [2, D, D])
    ml_win = din("ml_win", [2, D, 6152])
    ml_wout = din("ml_wout", [2, D, D])
    ml_bif0 = din("ml_bif0", [1, 16])
    ml_bif_tm = din("ml_bif_tm", [2, 16, 8])
    ml_gbc = din("ml_gbc", [2, 128, D])
    st_s5 = din("st_s5", [2, 2, 128, 64, NS])
    st_C = din("st_C", [2, NS, NH, DK, DV])
    st_n = din("st_n", [2, NS, NH * DK])
    st_m = din("st_m", [2, NS, NH])
    d_consts = din("consts", [128, CW])

    sc_gate = nc.dram_tensor("sc_gate", [DEPTH, 2, D, FF], BF16, kind="Internal").ap()
    sc_up = nc.dram_tensor("sc_up", [DEPTH, 2, D, FF], BF16, kind="Internal").ap()
    sc_down = nc.dram_tensor("sc_down", [DEPTH, 2, FF, D], BF16, kind="Internal").ap()
    yT_p = dout("yT_p", [D, SEQ])
    yT_s = dout("yT_s", [D, NS])
    o_p_s5 = dout("o_p_s5", [2, 2, 128, 64])
    o_p_C = dout("o_p_C", [2, NH, DK, DV])
    o_p_n = dout("o_p_n", [2, 128, 8])
    o_p_m = dout("o_p_m", [2, 4, 1])
    o_s_s5 = dout("o_s_s5", [2, 2, 128, 64, NS])
    o_s_C = dout("o_s_C", [2, NS, NH, DK, DV])
    o_s_n = dout("o_s_n", [2, NS, NH * DK])
    o_s_m = dout("o_s_m", [2, NS, NH])

    with contextlib.ExitStack() as es:
        B = Builder(nc, es)

        def sb(name, shape, dt):
            return es.enter_context(nc.sbuf_tensor("sb_" + name, list(shape), dt))

        PE = Eng(nc.tensor, B.sem("pe"), is_pe=True)
        ACT = Eng(nc.scalar, B.sem("act"))
        DVE = Eng(nc.vector, B.sem("dve"))
        SP = Eng(nc.sync, None)
        GQ = Eng(nc.gpsimd, None)

        x32 = sb("x32", [128, KT, NP], F32)
        xb = sb("xb", [128, KT, NP], BF16)
        arena = sb("arena", [128, 32768], BF16)
        AR = Reg("arena")
        wsl = [sb(f"wsl{i}", [128, 8192], BF16) for i in range(3)]
        wsl_reg = [Reg(f"wsl{i}") for i in range(3)]
        wsl_sem = [B.sem(f"wsl{i}") for i in range(3)]
        consts = sb("consts", [128, CW], F32)
        cb16 = sb("cb16", [128, 512], BF16)
        cbias = sb("cbias", [128, 4], F32)
        lng = sb("lng", [128, 192], F32)
        lnb = sb("lnb", [128, 192], F32)
        s5d = sb("s5d", [128, 32], F32)
        hst = sb("hst", [128, 2, 2, 64], F32)
        n32 = sb("n32", [128, 2, 8], F32)
        mrow = sb("mrow", [1, 8], F32)
        mlb0 = sb("mlb0", [1, 16], F32)
        rw = sb("rw", [1, 2048], F32)
        RWr = Reg("rw")
        PCr = [Reg(f"pc{i}") for i in range(8)]
        csem = B.sem("cld")
        reserved = set()
        S5W = Reg("s5w")
        SCR = Reg("scratch_w")
        scsem = B.sem("scst")
        wmode = ["store"]
        CR = Reg("consts")
        x32r = [Reg(f"x32_{k}") for k in range(KT)]
        xbr = [Reg(f"xb_{k}") for k in range(KT)]
        hstR = Reg("hst")
        nR = Reg("n32")
        mR = Reg("mrow")
        ldsem = B.sem("ld")
        stsem = B.sem("st")
        out_sems = [stsem]

        ps_all = es.enter_context(nc.psum_tensor("ps", [128, 8, 512], F32))
        banks = [Bank(ps_all[:, i, :], Reg(f"bank{i}")) for i in range(8)]
        bank_ctr = [0]

        def nb():
            while (bank_ctr[0] % 8) in reserved:
                bank_ctr[0] += 1
            b = banks[bank_ctr[0] % 8]
            bank_ctr[0] += 1
            return b

        slot_ctr = [0]

        def next_slot():
            i = slot_ctr[0] % 3
            slot_ctr[0] += 1
            return wsl[i], wsl_reg[i], wsl_sem[i]

        def av(off, n, dt=F32):
            if dt == F32:
                return arena[:, off // 2: off // 2 + 2 * n].bitcast(F32)
            if dt == I32:
                return arena[:, off // 2: off // 2 + 2 * n].bitcast(I32)
            return arena[:, off // 2: off // 2 + n]

        def V_tt(out, a, b, op, reads, writes):
            return B.op(DVE, lambda: nc.vector.tensor_tensor(out=out, in0=a, in1=b, op=op), reads, writes)

        def V_ts(out, a, s1, op0, reads, writes, s2=None, op1=None):
            if op1 is None:
                return B.op(DVE, lambda: nc.vector.tensor_scalar(out=out, in0=a, scalar1=s1, scalar2=None, op0=op0), reads, writes)
            return B.op(DVE, lambda: nc.vector.tensor_scalar(out=out, in0=a, scalar1=s1, scalar2=s2, op0=op0, op1=op1), reads, writes)

        def V_stt(out, a, s, b, op0, op1, reads, writes):
            return B.op(DVE, lambda: nc.vector.scalar_tensor_tensor(out=out, in0=a, scalar=s, in1=b, op0=op0, op1=op1), reads, writes)

        def V_cp(out, a, reads, writes):
            return B.op(DVE, lambda: nc.vector.tensor_copy(out=out, in_=a), reads, writes)

        def A_f(out, a, func, reads, writes, bias=None, scale=None):
            kw = {}
            if bias is not None:
                kw["bias"] = bias
            if scale is not None:
                kw["scale"] = scale
            return B.op(ACT, lambda: nc.scalar.activation(out=out, in_=a, func=func, **kw), reads, writes)

        def MM(bank, out, lhsT, rhs, reads, start, stop):
            return B.op(PE, lambda: nc.tensor.matmul(out, lhsT, rhs, start=start, stop=stop),
                        reads, [bank.reg], sig=True)

        def TR(bank, out, in_, ident, reads):
            return B.op(PE, lambda: nc.tensor.transpose(out, in_, ident), reads, [bank.reg], sig=True)

        B.dma(SP, consts[:], d_consts[:, :], [], [CR], ldsem)
        B.dma(SP, lng[:], d_lng[:, :], [], [CR], ldsem)
        B.dma(SP, lnb[:], d_lnb[:, :], [], [CR], ldsem)
        B.dma(SP, s5d[:], d_s5d[:, :], [], [CR], ldsem)
        B.dma(SP, mlb0[:], ml_bif0[:, :], [], [CR], ldsem)
        V_cp(cb16[:, 0:128], consts[:, C_ID:C_ID + 128], [CR], [CR])
        V_cp(cb16[:, 128:256], consts[:, C_ONES:C_ONES + 128], [CR], [CR])
        B.op(DVE, lambda: nc.vector.memset(cbias[:, 0:1], math.pi / 2.0), [], [CR])
        B.op(DVE, lambda: nc.vector.memset(cbias[:, 1:2], EPS_LN), [], [CR])
        B.op(DVE, lambda: nc.vector.memset(cbias[:, 2:3], EPS), [], [CR])
        B.op(DVE, lambda: nc.vector.memset(cbias[:, 3:4], 1.0), [], [CR])
        B.op(DVE, lambda: nc.vector.memset(hst[:], 0.0), [], [hstR])
        B.op(DVE, lambda: nc.vector.memset(n32[:], 0.0), [], [nR])
        B.op(DVE, lambda: nc.vector.memset(mrow[:], 0.0), [], [mR])
        ident32 = consts[:, C_ID:C_ID + 128]
        identb = cb16[:, 0:128]
        onesb = cb16[:, 128:256]
        maskT = consts[:, C_MASKT:C_MASKT + 128]
        iota1 = consts[:, C_IOTA:C_IOTA + 512]
        I16 = consts[0:16, C_I16:C_I16 + 16]
        ones32 = consts[:, C_ONES:C_ONES + 128]

        def stream_fm(mats, ncols, N, src, evac, cw=256, scr=None):
            nm = len(mats)
            c0 = 0
            while c0 < ncols:
                w = min(cw, ncols - c0)
                st, sr, ss = next_slot()
                views = []
                for j, m in enumerate(mats):
                    v = st[:, j * 4096: j * 4096 + KT * w].rearrange("p (k c) -> p k c", k=KT)
                    if scr is not None and wmode[0] == "load":
                        B.dma(GQ, v, scr[j][:, c0:c0 + w].rearrange("(k p) c -> p k c", p=128), [SCR], [sr], ss)
                    else:
                        B.dma(GQ, v, m[:, c0:c0 + w].rearrange("(k p) c -> p k c", p=128), [], [sr], ss)
                        if scr is not None:
                            B.dma(SP, scr[j][:, c0:c0 + w].rearrange("(k p) c -> p k c", p=128), v, [sr], [SCR], scsem)
                    views.append(v)
                for mi in range(w // 128):
                    bs = []
                    for j in range(nm):
                        bk = nb()
                        for kt in range(KT):
                            sap, sreg = src(kt)
                            MM(bk, bk.ap[:, :N], views[j][:, kt, mi * 128:(mi + 1) * 128], sap, [sr, sreg],
                               kt == 0, kt == KT - 1)
                        bs.append(bk)
                    evac(c0 // 128 + mi, bs)
                c0 += w

        def stream_tm(mat, c_lo, nchunks, tchunks, evac):
            for cc in range(nchunks):
                st, sr, ss = next_slot()
                v = st[:, 0:KT * 512].rearrange("p (k c) -> p k c", k=KT)
                B.dma(GQ, v, mat[:, c_lo + cc * 512: c_lo + (cc + 1) * 512].rearrange("(k p) c -> p k c", p=128),
                      [], [sr], ss)
                for ti, (t0, nt) in enumerate(tchunks):
                    bk = nb()
                    for kt in range(KT):
                        MM(bk, bk.ap[:nt, :], xb[:, kt, t0:t0 + nt], v[:, kt, :], [sr, xbr[kt]], kt == 0, kt == KT - 1)
                    evac(cc, ti, bk)

        def src_xb(N):
            return lambda kt: (xb[:, kt, :N], xbr[kt])

        def layer_norm(l, i, N):
            zb = av(0, KT * NP, BF16).rearrange("p (k n) -> p k n", k=KT)
            z2b = av(16384, KT * NP, BF16).rearrange("p (k n) -> p k n", k=KT)
            mean = av(32768, NP)
            msq = av(32768 + 2048, NP)
            rstd = av(32768 + 4096, NP)
            nmr = av(32768 + 6144, NP)
            A_f(zb[:, :, :N], x32[:, :, :N], AF.Copy, x32r, [AR])
            A_f(z2b[:, :, :N], x32[:, :, :N], AF.Square, x32r, [AR])
            b1 = nb()
            b2 = nb()
            for kt in range(KT):
                MM(b1, b1.ap[:, :N], onesb, zb[:, kt, :N], [AR, CR], kt == 0, kt == KT - 1)
            for kt in range(KT):
                MM(b2, b2.ap[:, :N], onesb, z2b[:, kt, :N], [AR, CR], kt == 0, kt == KT - 1)
            V_ts(mean[:, :N], b1.ap[:, :N], 1.0 / D, ALU.mult, [b1.reg], [AR])
            V_tt(msq[:, :N], mean[:, :N], mean[:, :N], ALU.mult, [AR], [AR])
            V_stt(rstd[:, :N], b2.ap[:, :N], 1.0 / D, msq[:, :N], ALU.mult, ALU.subtract, [b2.reg, AR], [AR])
            A_f(rstd[:, :N], rstd[:, :N], AF.Sqrt, [AR, CR], [AR], bias=cbias[:, 1:2])
            B.op(DVE, lambda: nc.vector.reciprocal(out=rstd[:, :N], in_=rstd[:, :N]), [AR], [AR])
            V_cp(b1.ap[:, :N], rstd[:, :N], [AR, b1.reg], [b1.reg])
            V_stt(b2.ap[:, :N], mean[:, :N], -1.0, rstd[:, :N], ALU.mult, ALU.mult, [AR, b2.reg], [b2.reg])
            gi = (l * 3 + i) * KT
            V_tt(x32[:, :, :N], x32[:, :, :N], b1.ap[:, :N].unsqueeze(1).to_broadcast([128, KT, N]), ALU.mult,
                 x32r + [b1.reg], x32r)
            V_tt(x32[:, :, :N], x32[:, :, :N], b2.ap[:, :N].unsqueeze(1).to_broadcast([128, KT, N]), ALU.add,
                 x32r + [b2.reg], x32r)
            for kt in range(KT):
                A_f(x32[:, kt, :N], x32[:, kt, :N], AF.Identity, [x32r[kt], CR], [x32r[kt]],
                    bias=lnb[:, gi + kt:gi + kt + 1], scale=lng[:, gi + kt:gi + kt + 1])
            A_f(xb[:, :, :N], x32[:, :, :N], AF.Copy, x32r, xbr)

        def ffn_s(l, i):
            N = NS
            R = [AR]
            g_sb = av(0, FF)[0:16]
            h_tm = av(22528, FF, BF16)[0:16]
            hT = av(34816, MT * NS, BF16).rearrange("p (m n) -> p m n", m=MT)
            y_tm = av(36864, D)[0:16]
            chunks = [(c0, min(512, FF - c0)) for c0 in range(0, FF, 512)]
            for which, wmat in enumerate((w_gate, w_up)):
                for (c0, w) in chunks:
                    st, sr, ss = next_slot()
                    v = st[:, 0:KT * w].rearrange("p (k c) -> p k c", k=KT)
                    scm = (sc_gate, sc_up)[which]
                    if wmode[0] == "load":
                        B.dma(GQ, v, scm[l, i][:, c0:c0 + w].rearrange("(k p) c -> p k c", p=128), [SCR], [sr], ss)
                    else:
                        B.dma(GQ, v, wmat[l, i][:, c0:c0 + w].rearrange("(k p) c -> p k c", p=128), [], [sr], ss)
                    bk = nb()
                    for kt in range(KT):
                        MM(bk, bk.ap[0:16, 0:w], xb[:, kt, 0:16], v[:, kt, :], [sr, xbr[kt]], kt == 0, kt == KT - 1)
                    if which == 0:
                        A_f(g_sb[:, c0:c0 + w], bk.ap[0:16, 0:w], AF.Silu, [bk.reg] + R, R)
                    else:
                        V_tt(h_tm[:, c0:c0 + w], g_sb[:, c0:c0 + w], bk.ap[0:16, 0:w], ALU.mult, [bk.reg] + R, R)
            bt = nb()
            btb = bt.ap.bitcast(BF16)
            for m in range(MT):
                TR(bt, btb[:, m * 16:(m + 1) * 16], h_tm[:, m * 128:(m + 1) * 128], identb[0:16, 0:16], R + [CR])
            A_f(hT, btb[:, 0:MT * 16].rearrange("p (m n) -> p m n", m=MT), AF.Copy, [bt.reg] + R, R)
            fchunks = [(f0, min(16, MT - f0)) for f0 in range(0, MT, 16)]
            for dg in range(4):
                bk = nb()
                for (f0, nf) in fchunks:
                    st, sr, ss = next_slot()
                    v = st[:, 0:nf * 512].rearrange("p (f c) -> p f c", f=nf)
                    if wmode[0] == "load":
                        B.dma(GQ, v, sc_down[l, i, f0 * 128:(f0 + nf) * 128, dg * 512:(dg + 1) * 512]
                              .rearrange("(f p) c -> p f c", p=128), [SCR], [sr], ss)
                    else:
                        B.dma(GQ, v, w_down[l, i, f0 * 128:(f0 + nf) * 128, dg * 512:(dg + 1) * 512]
                              .rearrange("(f p) c -> p f c", p=128), [], [sr], ss)
                    for fi in range(nf):
                        f = f0 + fi
                        MM(bk, bk.ap[0:16, :], hT[:, f, :], v[:, fi, :], [sr, AR], f == 0, f == MT - 1)
                A_f(y_tm[:, dg * 512:(dg + 1) * 512], bk.ap[0:16, :], AF.Copy, [bk.reg] + R, R)
            by = nb()
            for kt in range(KT):
                TR(by, by.ap[:, kt * 16:(kt + 1) * 16], y_tm[:, kt * 128:(kt + 1) * 128], ident32[0:16, 0:16], R + [CR])
            V_stt(x32[:, :, :N], by.ap[:, 0:KT * 16].rearrange("p (k n) -> p k n", k=KT), 0.5 / ALPHA, x32[:, :, :N],
                  ALU.mult, ALU.add, [by.reg] + x32r, x32r)
            layer_norm(l, 2 * i, N)

        def ffn(l, i, N):
            if N == NS:
                return ffn_s(l, i)
            hT = av(0, MT * NP, BF16).rearrange("p (m n) -> p m n", m=MT)
            hTr = [Reg(f"hT{m}") for m in range(MT)]
            sgs = [av(44032 + 2048 * j, NP) for j in range(2)]
            sgr = [Reg("sg0"), Reg("sg1")]
            for m in range(MT):
                hTr[m].w = AR.w
                hTr[m].rs = dict(AR.rs)
            for j in range(2):
                sgr[j].w = AR.w
                sgr[j].rs = dict(AR.rs)
            cnt = [0]

            def evac(m, bs):
                j = cnt[0] % 2
                cnt[0] += 1
                A_f(sgs[j][:, :N], bs[0].ap[:, :N], AF.Silu, [bs[0].reg], [sgr[j]])
                V_tt(hT[:, m, :N], sgs[j][:, :N], bs[1].ap[:, :N], ALU.mult, [sgr[j], bs[1].reg], [hTr[m]])

            stream_fm([w_gate[l, i], w_up[l, i]], FF, N, src_xb(N), evac, scr=[sc_gate[l, i], sc_up[l, i]])
            fchunks = [(f0, min(16, MT - f0)) for f0 in range(0, MT, 16)]
            for dg in range(4):
                bks = [nb() for _ in range(4)]
                for (f0, nf) in fchunks:
                    st, sr, ss = next_slot()
                    v = st[:, 0:nf * 512].rearrange("p (f c) -> p f c", f=nf)
                    scv = sc_down[l, i, f0 * 128:(f0 + nf) * 128, dg * 512:(dg + 1) * 512].rearrange("(f p) c -> p f c", p=128)
                    if wmode[0] == "load":
                        B.dma(GQ, v, scv, [SCR], [sr], ss)
                    else:
                        B.dma(GQ, v, w_down[l, i, f0 * 128:(f0 + nf) * 128, dg * 512:(dg + 1) * 512]
                              .rearrange("(f p) c -> p f c", p=128), [], [sr], ss)
                        B.dma(SP, scv, v, [sr], [SCR], scsem)
                    for fi in range(nf):
                        f = f0 + fi
                        for dt in range(4):
                            MM(bks[dt], bks[dt].ap[:, :N], v[:, fi, dt * 128:(dt + 1) * 128], hT[:, f, :N],
                               [sr, hTr[f]], f == 0, f == MT - 1)
                for dt in range(4):
                    kt = dg * 4 + dt
                    V_stt(x32[:, kt, :N], bks[dt].ap[:, :N], 0.5 / ALPHA, x32[:, kt, :N], ALU.mult, ALU.add,
                          [bks[dt].reg, x32r[kt]], [x32r[kt]])
            collect_arena(hTr + sgr)
            layer_norm(l, 2 * i, N)

        def collect_arena(regs):
            ws = {}
            for r in regs:
                if r.w is not None:
                    s, v = r.w
                    ws[s] = max(ws.get(s, 0), v)
                for s, v in r.rs.items():
                    ws[s] = max(ws.get(s, 0), v)
            if AR.w is not None:
                s, v = AR.w
                ws[s] = max(ws.get(s, 0), v)
            for s, v in ws.items():
                AR.rs[s] = max(AR.rs.get(s, 0), v)

        def sincos(th, n_el, shape_fn, sn, cs, ki, fr, regs):
            V_cp(ki, th, regs, regs)
            V_tt(fr, th, ki, ALU.subtract, regs, regs)
            A_f(sn, fr, AF.Sin, regs, regs, scale=SIN_SCALE)
            A_f(fr, fr, AF.Abs, regs, regs)
            A_f(cs, fr, AF.Sin, regs + [CR], regs, scale=-SIN_SCALE, bias=cbias[:, 0:1])

        def s5_mix(l, j, N, is_s):
            R = [AR]
            Bblk = av(0, KT * 2 * 128, BF16).rearrange("p (k r m) -> p k r m", k=KT, r=2)
            Cblk = av(8192, 64 * 2 * 32, BF16).rearrange("p (i r m) -> p i r m", i=64, r=2)
            zb = av(16384, KT * NP, BF16).rearrange("p (k n) -> p k n", k=KT)
            thn = av(32768, 64)
            rho = av(32768 + 256, 64)
            abr = av(32768 + 512, 64)
            abi = av(32768 + 768, 64)
            nabi = av(32768 + 1024, 64)
            P0 = 34816
            eoffs = [16384, 20480, 24576, 28672, P0, P0 + 4096, P0 + 8192, P0 + 12288]

            def e(k, dt=F32):
                return av(eoffs[k], KT * PS, dt).rearrange("p (k s) -> p k s", k=KT)
            ar_, ai_, t0, t1, t2, t3, t4 = e(0), e(1), e(2), e(3), e(4), e(5), e(6)
            kiE = e(7, I32)
            dtE = av(P0 + 16384, KT)
            B.dma(SP, ar_, aE_re[j], [], R, ldsem)
            B.dma(SP, ai_, aE_im[j], [], R, ldsem)
            B.dma(SP, dtE, ldtE[j], [], R, ldsem)
            A_f(dtE, dtE, AF.Exp, R, R)
            dtb = dtE.unsqueeze(2).to_broadcast([128, KT, PS])
            V_tt(t0, ar_, dtb, ALU.mult, R, R)
            A_f(t0, t0, AF.Exp, R, R)
            V_tt(t1, ai_, dtb, ALU.mult, R, R)
            V_ts(t1, t1, 1.0 / TWO_PI, ALU.mult, R, R)
            sincos(t1, None, None, t2, t3, kiE, t4, R)
            V_tt(t2, t2, t0, ALU.mult, R, R)
            V_tt(t3, t3, t0, ALU.mult, R, R)
            V_ts(t3, t3, -1.0, ALU.add, R, R)
            V_tt(t0, ar_, ar_, ALU.mult, R, R)
            V_tt(t1, ai_, ai_, ALU.mult, R, R)
            V_tt(t0, t0, t1, ALU.add, R, R)
            B.op(DVE, lambda: nc.vector.reciprocal(out=t0, in_=t0), R, R)
            V_tt(t1, t3, ar_, ALU.mult, R, R)
            V_tt(t4, t2, ai_, ALU.mult, R, R)
            V_tt(t1, t1, t4, ALU.add, R, R)
            V_tt(t1, t1, t0, ALU.mult, R, R)
            V_tt(t4, t2, ar_, ALU.mult, R, R)
            V_tt(t3, t3, ai_, ALU.mult, R, R)
            V_tt(t4, t4, t3, ALU.subtract, R, R)
            V_tt(t4, t4, t0, ALU.mult, R, R)
            B.dma(SP, ar_, bE_re[j], [], R, ldsem)
            B.dma(SP, ai_, bE_im[j], [], R, ldsem)
            V_tt(t0, t1, ar_, ALU.mult, R, R)
            V_tt(t2, t4, ai_, ALU.mult, R, R)
            V_tt(t0, t0, t2, ALU.subtract, R, R)
            V_tt(t2, t1, ai_, ALU.mult, R, R)
            V_tt(t3, t4, ar_, ALU.mult, R, R)
            V_tt(t2, t2, t3, ALU.add, R, R)
            for g2 in range(2):
                V_ts(Bblk[:, :, 0, g2 * 64:(g2 + 1) * 64], t0, consts[:, C_MG2 + g2:C_MG2 + g2 + 1], ALU.mult, R + [CR], R + [S5W])
                V_ts(Bblk[:, :, 1, g2 * 64:(g2 + 1) * 64], t2, consts[:, C_MG2 + g2:C_MG2 + g2 + 1], ALU.mult, R + [CR], R + [S5W])
            hA = av(P0, 64)
            hB = av(P0 + 256, 64)
            hC = av(P0 + 512, 64)
            hD = av(P0 + 768, 64)
            hK = av(P0 + 1024, 64, I32)
            cT = av(P0 + 2048, 64 * 16).rearrange("p (i c) -> p i c", i=64)
            B.dma(SP, hA, aH_re[j], [], R, ldsem)
            B.dma(SP, hB, aH_im[j], [], R, ldsem)
            B.dma(SP, hC, ldtH[j], [], R, ldsem)
            A_f(hC, hC, AF.Exp, R, R)
            V_tt(rho, hA, hC, ALU.mult, R, R)
            A_f(rho, rho, AF.Exp, R, R)
            V_tt(hB, hB, hC, ALU.mult, R, R)
            V_ts(hB, hB, 1.0 / TWO_PI, ALU.mult, R, R)
            V_cp(hK, hB, R, R)
            V_tt(thn, hB, hK, ALU.subtract, R, R)
            A_f(hA, thn, AF.Sin, R, R, scale=SIN_SCALE)
            A_f(hD, thn, AF.Abs, R, R)
            A_f(hD, hD, AF.Sin, R + [CR], R, scale=-SIN_SCALE, bias=cbias[:, 0:1])
            V_tt(abi, hA, rho, ALU.mult, R, R)
            V_tt(abr, hD, rho, ALU.mult, R, R)
            V_ts(nabi, abi, -1.0, ALU.mult, R, R)
            B.dma(SP, cT, cH_re[j], [], R, ldsem)
            for g2 in range(2):
                V_ts(Cblk[:, :, 0, g2 * 16:(g2 + 1) * 16], cT, consts[:, C_MH + g2:C_MH + g2 + 1], ALU.mult, R + [CR], R + [S5W])
            B.dma(SP, cT, cH_im[j], [], R, ldsem)
            for g2 in range(2):
                V_ts(Cblk[:, :, 1, g2 * 16:(g2 + 1) * 16], cT, consts[:, C_MH + g2:C_MH + g2 + 1], ALU.mult, R + [CR], R + [S5W],
                     s2=-1.0, op1=ALU.mult)
            kf = av(P0, NP)
            kfi = av(P0, NP, I32)
            sn = av(P0 + 2048, NP)
            cs = av(P0 + 4096, NP)
            u1 = av(P0 + 6144, NP)
            u2 = av(P0 + 8192, NP)
            gr = av(P0 + 10240, NP)
            gi = av(P0 + 12288, NP)
            ysb = gi
            Cq3 = av(49152, KT * 2 * 64, BF16).rearrange("p (k r m) -> p k r m", k=KT, r=2)
            B.op(DVE, lambda: nc.vector.memset(Cq3, 0.0), R, R + [S5W])
            for kt_ in range(KT):
                V_cp(Cq3[:, kt_, :, 32:64], Cblk[:, 4 * kt_ + 3, :, :], R, R + [S5W])
            HrA = [av(53248, NP, BF16), av(55296, NP, BF16)]
            HiA = [av(54272, NP, BF16), av(56320, NP, BF16)]
            Bq3 = av(57344, KT * 2 * 128, BF16).rearrange("p (k r m) -> p k r m", k=KT, r=2)
            V_ts(Bq3, Bblk, consts[:, C_M96:C_M96 + 1], ALU.mult, R + [CR], R + [S5W])
            Rt, Ru, Ry, Rz = Reg("s5t"), Reg("s5u"), Reg("s5y"), Reg("s5z")
            RH = [Reg("s5h0"), Reg("s5h1")]
            for r_ in [Rt, Ru, Ry, Rz] + RH:
                r_.w = AR.w
                r_.rs = dict(AR.rs)
            if is_s:
                u1 = av(P0, NS)
                u2 = av(P0 + 64, NS)
                zb = av(16384, KT * NS, BF16).rearrange("p (k n) -> p k n", k=KT)
                h0 = av(16896, 2 * 64 * NS).rearrange("p (r i b) -> p r i b", r=2, i=64)
                B.dma(SP, h0[:, 0], st_s5[j, 0], [], [Ru], ldsem)
                B.dma(SP, h0[:, 1], st_s5[j, 1], [], [Ru], ldsem)
                hn_ = av(34944, 2 * 64 * NS).rearrange("p (r i b) -> p r i b", r=2, i=64)
            ybank = None

            def emitB(ii_):
                kt_b = ii_ // 4
                q_b = (0, 1, 3, 2)[ii_ % 4]
                prb = slice(32 * q_b, 32 * q_b + 32)
                b_r = nb()
                b_i = nb()
                if q_b < 3:
                    MM(b_r, b_r.ap[:, :N], Bblk[prb, kt_b, 0, :], xb[prb, kt_b, :N], [S5W, xbr[kt_b]], True, True)
                    MM(b_i, b_i.ap[:, :N], Bblk[prb, kt_b, 1, :], xb[prb, kt_b, :N], [S5W, xbr[kt_b]], True, True)
                else:
                    MM(b_r, b_r.ap[:, :N], Bq3[64:128, kt_b, 0, :], xb[64:128, kt_b, :N], [S5W, xbr[kt_b]], True, True)
                    MM(b_i, b_i.ap[:, :N], Bq3[64:128, kt_b, 1, :], xb[64:128, kt_b, :N], [S5W, xbr[kt_b]], True, True)
                return b_r, b_i

            nxtB = emitB(0)
            for ii in range(64):
                kt = ii // 4
                q = (0, 1, 3, 2)[ii % 4]
                i = kt * 4 + q
                pr = slice(32 * q, 32 * q + 32)
                Hr, Hi, RHk = HrA[ii % 2], HiA[ii % 2], RH[ii % 2]
                br_, bi_ = nxtB
                if not is_s:
                    B.op(DVE, lambda: nc.vector.tensor_scalar(out=kfi[:, :N], in0=iota1[:, :N], scalar1=thn[:, i:i + 1],
                                                              scalar2=None, op0=ALU.mult), [AR, CR], [Rt])
                    V_stt(kf[:, :N], iota1[:, :N], thn[:, i:i + 1], kfi[:, :N], ALU.mult, ALU.subtract, [AR, CR, Rt], [Rt])
                    A_f(sn[:, :N], kf[:, :N], AF.Sin, [Rt], [Rt], scale=SIN_SCALE)
                    A_f(kf[:, :N], kf[:, :N], AF.Abs, [Rt], [Rt])
                    A_f(cs[:, :N], kf[:, :N], AF.Sin, [Rt, CR], [Rt], scale=-SIN_SCALE, bias=cbias[:, 0:1])
                    V_tt(u1[:, :N], cs[:, :N], br_.ap[:, :N], ALU.mult, [Rt, br_.reg], [Ru])
                    V_tt(u2[:, :N], sn[:, :N], bi_.ap[:, :N], ALU.mult, [Rt, bi_.reg], [Ru])
                    V_tt(u1[:, :N], u1[:, :N], u2[:, :N], ALU.add, [Ru], [Ru])
                    V_tt(u2[:, :N], cs[:, :N], bi_.ap[:, :N], ALU.mult, [Rt, bi_.reg], [Ru])
                    V_tt(gr[:, :N], sn[:, :N], br_.ap[:, :N], ALU.mult, [Rt, br_.reg], [Ru])
                    V_tt(u2[:, :N], u2[:, :N], gr[:, :N], ALU.subtract, [Ru], [Ru])
                    rb = rho[:, i:i + 1].to_broadcast([128, N])
                    B.op(DVE, lambda: nc.vector.tensor_tensor_scan(out=gr[:, :N], data0=rb, data1=u1[:, :N],
                                                                   initial=hst[:, j, 0, i:i + 1], op0=ALU.mult, op1=ALU.add),
                         [AR, Ru, hstR], [Ru])
                    B.op(DVE, lambda: nc.vector.tensor_tensor_scan(out=gi[:, :N], data0=rb, data1=u2[:, :N],
                                                                   initial=hst[:, j, 1, i:i + 1], op0=ALU.mult, op1=ALU.add),
                         [AR, Ru, hstR], [Ru, Ry])
                    V_tt(u1[:, :N], cs[:, :N], gr[:, :N], ALU.mult, [Rt, Ru], [Ru])
                    V_tt(u2[:, :N], sn[:, :N], gi[:, :N], ALU.mult, [Rt, Ru], [Ru])
                    V_tt(Hr[:, :N], u1[:, :N], u2[:, :N], ALU.subtract, [Ru], [RHk])
                    V_stt(hst[:, j, 0, i:i + 1], u1[:, N - 1:N], 1.0, u2[:, N - 1:N], ALU.mult, ALU.subtract, [Ru, hstR], [hstR])
                    V_tt(u1[:, :N], cs[:, :N], gi[:, :N], ALU.mult, [Rt, Ru], [Ru])
                    V_tt(u2[:, :N], sn[:, :N], gr[:, :N], ALU.mult, [Rt, Ru], [Ru])
                    V_tt(Hi[:, :N], u1[:, :N], u2[:, :N], ALU.add, [Ru], [RHk])
                    V_stt(hst[:, j, 1, i:i + 1], u1[:, N - 1:N], 1.0, u2[:, N - 1:N], ALU.mult, ALU.add, [Ru, hstR], [hstR])
                else:
                    V_ts(u1[:, :N], h0[:, 0, i, :], abr[:, i:i + 1], ALU.mult, [AR, Ru], [Ru])
                    V_stt(u1[:, :N], h0[:, 1, i, :], nabi[:, i:i + 1], u1[:, :N], ALU.mult, ALU.add, [AR, Ru], [Ru])
                    V_tt(hn_[:, 0, i, :], u1[:, :N], br_.ap[:, :N], ALU.add, [Ru, br_.reg], [Ru])
                    V_ts(u2[:, :N], h0[:, 1, i, :], abr[:, i:i + 1], ALU.mult, [AR, Ru], [Ru])
                    V_stt(u2[:, :N], h0[:, 0, i, :], abi[:, i:i + 1], u2[:, :N], ALU.mult, ALU.add, [AR, Ru], [Ru])
                    V_tt(hn_[:, 1, i, :], u2[:, :N], bi_.ap[:, :N], ALU.add, [Ru, bi_.reg], [Ru])
                    V_cp(Hr[:, :N], hn_[:, 0, i, :], [Ru], [RHk])
                    V_cp(Hi[:, :N], hn_[:, 1, i, :], [Ru], [RHk])
                if ii + 1 < 64:
                    nxtB = emitB(ii + 1)
                if q == 0:
                    ybank = nb()
                    reserved.add(banks.index(ybank))
                if q == 3:
                    ylh0, ylh1, yout, yst = Cq3[:, kt, 0, :], Cq3[:, kt, 1, :], ybank.ap[64:128, :N], True
                else:
                    ylh0, ylh1, yout, yst = Cblk[:, i, 0, :], Cblk[:, i, 1, :], ybank.ap[pr, :N], q != 2
                B.op(PE, lambda: nc.tensor.matmul(yout, ylh0, Hr[:, :N], start=yst, stop=False), [S5W, RHk], [ybank.reg])
                B.op(PE, lambda: nc.tensor.matmul(yout, ylh1, Hi[:, :N], start=False, stop=True), [S5W, RHk], [ybank.reg])
                if ii % 4 == 3:
                    V_stt(ysb[:, :N], x32[:, kt, :N], s5d[:, j * KT + kt:j * KT + kt + 1], ybank.ap[:, :N],
                          ALU.mult, ALU.add, [x32r[kt], ybank.reg, CR, Ru], [Ru, Ry])
                    A_f(zb[:, kt, :N], ysb[:, :N], AF.Gelu_apprx_tanh, [Ry], [Rz])
                    reserved.discard(banks.index(ybank))
            if is_s:
                for r_ in range(2):
                    B.dma(SP, o_s_s5[j, r_], hn_[:, r_], [Ru], [], stsem)
            collect_arena([S5W, Rt, Ru, Ry, Rz] + RH)
            AR.w = Rz.w if Rz.w is not None else AR.w
            R = [AR]
            sg = av(P0, NP)
            pr2 = av(P0 + 2048, NP)

            def evac(dt, bs):
                A_f(sg[:, :N], bs[1].ap[:, :N], AF.Sigmoid, [bs[1].reg] + R, R)
                V_tt(pr2[:, :N], sg[:, :N], bs[0].ap[:, :N], ALU.mult, [bs[0].reg] + R, R)
                V_stt(x32[:, dt, :N], pr2[:, :N], 1.0 / ALPHA, x32[:, dt, :N], ALU.mult, ALU.add,
                      R + [x32r[dt]], [x32r[dt]])

            stream_fm([s5_wa[j], s5_wb[j]], D, N, lambda kt: (zb[:, kt, :N], AR), evac)
            layer_norm(l, 1, N)

        def log_sigmoid(out, z, tmp, npart, R):
            A_f(tmp, z, AF.Abs, R, R)
            A_f(tmp, tmp, AF.Exp, R, R, scale=-1.0)
            A_f(tmp, tmp, AF.Ln, R + [CR], R, bias=cbias[0:npart, 3:4])
            V_ts(out, z, 0.0, ALU.min, R, R)
            V_tt(out, out, tmp, ALU.subtract, R, R)

        def head_norm(hs, np_, g_ap, out_ap, st6, mv, R):
            B.op(DVE, lambda: nc.vector.bn_stats(out=st6, in_=hs), R, R)
            B.op(DVE, lambda: nc.vector.bn_aggr(out=mv, in_=st6), R, R)
            A_f(mv[:, 1:2], mv[:, 1:2], AF.Sqrt, R + [CR], R, bias=cbias[0:np_, 2:3])
            B.op(DVE, lambda: nc.vector.reciprocal(out=mv[:, 1:2], in_=mv[:, 1:2]), R, R)
            V_ts(hs, hs, mv[:, 0:1], ALU.subtract, R, R, s2=mv[:, 1:2], op1=ALU.mult)
            V_tt(out_ap, hs, g_ap, ALU.mult, R, R)

        def ml_out_and_norm(l, j, N, hnT, sg):
            def evac_o(ct, bs):
                A_f(sg[:, :N], bs[0].ap[:, :N], AF.Sigmoid, [bs[0].reg, AR], [AR])
                V_tt(hnT[:, ct, :N], hnT[:, ct, :N], sg[:, :N], ALU.mult, [AR], [AR])

            stream_fm([ml_win[j][:, 4096:6144]], D, N, src_xb(N), evac_o)

            def evac_y(dt, bs):
                V_stt(x32[:, dt, :N], bs[0].ap[:, :N], 1.0 / ALPHA, x32[:, dt, :N], ALU.mult, ALU.add,
                      [bs[0].reg, x32r[dt]], [x32r[dt]])

            stream_fm([ml_wout[j]], D, N, lambda kt: (hnT[:, kt, :N], AR), evac_y)
            layer_norm(l, 1, N)

        def load_gate_w(j, off):
            wg = av(off, KT * 8, BF16).rearrange("p (k c) -> p k c", k=KT)
            wgf = av(off + 256, KT * 8).rearrange("p (k c) -> p k c", k=KT)
            with nc.allow_non_contiguous_dma(reason="tiny gate weights"):
                B.dma(SP, wgf, ml_win[j][:, 6144:6152].rearrange("(k p) c -> p k c", p=128), [], [AR], ldsem)
            V_cp(wg, wgf, [AR], [AR])
            return wg

        def mlstm_p(l, j, N, first_tile):
            R = [AR]
            NCH = N // 128
            qT = av(0, 2 * NP, BF16).rearrange("p (k n) -> p k n", k=2)
            kT = av(2048, 2 * NP, BF16).rearrange("p (k n) -> p k n", k=2)
            v_tm = av(4096, 4 * DV, BF16).rearrange("p (c f) -> p c f", c=4)
            C32 = av(8192, 2 * DV).rearrange("p (k v) -> p k v", k=2)
            Cb = av(12288, 2 * DV, BF16).rearrange("p (k v) -> p k v", k=2)
            gbc = av(14336, DV)
            wg = load_gate_w(j, 16384)
            S0 = 17408
            ET = av(S0, 128)
            scT = av(S0 + 512, 128, BF16)
            qs = av(S0 + 768, 256, BF16).rearrange("p (k n) -> p k n", k=2)
            kd = av(S0 + 1280, 256, BF16)
            hs = av(S0 + 1792, DV)
            hno = av(S0 + 3840, DV, BF16)
            cols = av(S0 + 4864, 4)
            bcs = av(S0 + 4880, 132)
            nbf = av(S0 + 5408, 2, BF16)
            st6 = av(S0 + 5412, 6)
            mv = av(S0 + 5436, 2)
            dsc = av(S0 + 5444, 1)
            sg = av(24576, NP)
            hnT = av(32768, KT * NP, BF16).rearrange("p (k n) -> p k n", k=KT)
            ig = rw[:, 0:512]
            lf = rw[:, 512:1024]
            ltmp = rw[:, 1024:1536]
            bcum = rw[:, 1024:1152]
            a_ = rw[:, 1152:1280]
            M_ = rw[:, 1280:1408]
            mt_ = rw[:, 1408:1536]
            dec_ = rw[:, 1536:1664]
            Rb = rw[:, 1664:1921]
            ones1 = consts[0:1, C_ONES:C_ONES + 128]
            one11 = consts[0:1, C_ONES:C_ONES + 1]
            for h in range(NH):
                jh = j * 4 + h
                bgi = nb()
                bgf = nb()
                for kt in range(KT):
                    MM(bgi, bgi.ap[0:1, :N], wg[:, kt, h:h + 1], xb[:, kt, :N], [AR, xbr[kt]], kt == 0, kt == KT - 1)
                for kt in range(KT):
                    MM(bgf, bgf.ap[0:1, :N], wg[:, kt, 4 + h:5 + h], xb[:, kt, :N], [AR, xbr[kt]], kt == 0, kt == KT - 1)
                A_f(ig[:, :N], bgi.ap[0:1, :N], AF.Identity, [bgi.reg, CR, RWr], [RWr], bias=mlb0[:, jh * 2:jh * 2 + 1])
                A_f(lf[:, :N], bgf.ap[0:1, :N], AF.Identity, [bgf.reg, CR, RWr], [RWr], bias=mlb0[:, jh * 2 + 1:jh * 2 + 2])
                log_sigmoid(lf[:, :N], lf[:, :N], ltmp[:, :N], 1, [RWr])

                def evac_q(ct, bs):
                    A_f(qT[:, ct, :N], bs[0].ap[:, :N], AF.Copy, [bs[0].reg] + R, R)

                def evac_k(ct, bs):
                    A_f(kT[:, ct, :N], bs[0].ap[:, :N], AF.Copy, [bs[0].reg] + R, R, scale=DK ** -0.5)

                stream_fm([ml_win[j][:, h * DK:(h + 1) * DK]], DK, N, src_xb(N), evac_q)
                stream_fm([ml_win[j][:, 1024 + h * DK:1024 + (h + 1) * DK]], DK, N, src_xb(N), evac_k)

                def evac_v(cc, ti, bk):
                    A_f(v_tm[:, ti, :], bk.ap[:, :], AF.Copy, [bk.reg] + R, R)

                stream_tm(ml_win[j], 2048 + h * DV, 1, [(c * 128, 128) for c in range(NCH)], evac_v)
                B.dma(SP, gbc, ml_gbc[j][:, h * DV:(h + 1) * DV], [], R, ldsem)
                if first_tile:
                    B.op(DVE, lambda: nc.vector.memset(C32, 0.0), R, R)
                else:
                    B.dma(SP, C32, o_p_C[j, h].rearrange("(k p) v -> p k v", p=128), [PCr[jh]], R, ldsem)
                A_f(Cb, C32, AF.Copy, R, R)
                for c in range(NCH):
                    tsl = slice(c * 128, (c + 1) * 128)
                    RR = [RWr]
                    B.op(DVE, lambda: nc.vector.tensor_tensor_scan(out=bcum, data0=ones1, data1=lf[:, tsl], initial=0.0,
                                                                   op0=ALU.mult, op1=ALU.add), RR + [CR], RR)
                    V_tt(a_, ig[:, tsl], bcum, ALU.subtract, RR, RR)
                    B.op(DVE, lambda: nc.vector.tensor_tensor_scan(out=M_, data0=ones1, data1=a_, initial=mrow[:, jh:jh + 1],
                                                                   op0=ALU.mult, op1=ALU.max), RR + [CR, mR], RR)
                    V_ts(Rb[:, 0:128], M_, -1.0, ALU.mult, RR, RR)
                    A_f(Rb[:, 128:256], Rb[:, 0:128], AF.Exp, RR + [mR], RR, bias=mrow[:, jh:jh + 1])
                    V_cp(Rb[:, 256:257], Rb[:, 255:256], RR, RR)
                    V_tt(mt_, bcum, M_, ALU.add, RR, RR)
                    V_cp(mrow[:, jh:jh + 1], mt_[:, 127:128], RR + [mR], [mR])
                    A_f(mt_, mt_, AF.Exp, RR, RR, scale=-1.0)
                    A_f(dec_, a_, AF.Exp, RR, RR, bias=Rb[:, 127:128])
                    bc_ = nb()
                    for ci, row in enumerate((a_, dec_, mt_)):
                        B.op(PE, lambda: nc.tensor.matmul(bc_.ap[:, ci:ci + 1], row, one11, start=True, stop=True),
                             RR + [CR], [bc_.reg])
                    V_cp(cols[:, 0:3], bc_.ap[:, 0:3], [bc_.reg] + R, R)
                    bb = nb()
                    B.op(PE, lambda: nc.tensor.matmul(bb.ap[:, 0:257], ones1, Rb, start=True, stop=True), RR + [CR], [bb.reg])
                    V_cp(bcs[:, 0:129], bb.ap[:, 128:257], [bb.reg] + R, R)
                    bs_ = nb()
                    for dk in range(2):
                        MM(bs_, bs_.ap[:, 0:128], kT[:, dk, tsl], qT[:, dk, tsl], R, dk == 0, dk == 1)
                    A_f(ET, bb.ap[:, 0:128], AF.Exp, [bb.reg] + R, R, bias=cols[:, 0:1])
                    V_tt(ET, ET, maskT, ALU.mult, R + [CR], R)
                    V_tt(scT, bs_.ap[:, 0:128], ET, ALU.mult, [bs_.reg] + R, R)
                    for dk in range(2):
                        V_tt(qs[:, dk, :], qT[:, dk, tsl], bcs[:, 0:128], ALU.mult, R, R)
                    bn_ = nb()
                    MM(bn_, bn_.ap[:, :], scT, v_tm[:, c, :], R, True, False)
                    for dk in range(2):
                        MM(bn_, bn_.ap[:, :], qs[:, dk, :], Cb[:, dk, :], R, False, dk == 1)
                    V_cp(nbf, n32[:, j, h * 2:h * 2 + 2], R + [nR], R)
                    bd_ = nb()
                    MM(bd_, bd_.ap[:, 0:1], scT, onesb[:, 0:1], R + [CR], True, False)
                    for dk in range(2):
                        MM(bd_, bd_.ap[:, 0:1], qs[:, dk, :], nbf[:, dk:dk + 1], R, False, dk == 1)
                    A_f(dsc, bd_.ap[:, 0:1], AF.Abs, [bd_.reg] + R, R)
                    V_tt(dsc, dsc, cols[:, 2:3], ALU.max, R, R)
                    B.op(DVE, lambda: nc.vector.reciprocal(out=dsc, in_=dsc), R, R)
                    A_f(hs, bn_.ap[:, :], AF.Copy, [bn_.reg] + R, R, scale=dsc[:, 0:1])
                    head_norm(hs, 128, gbc, hno, st6, mv, R)
                    bt = nb()
                    btb = bt.ap.bitcast(BF16)
                    for jj in range(4):
                        TR(bt, btb[:, jj * 128:(jj + 1) * 128], hno[:, jj * 128:(jj + 1) * 128], identb, R + [CR])
                    A_f(hnT[:, h * 4:(h + 1) * 4, tsl], btb[:, 0:512].rearrange("p (a t) -> p a t", a=4), AF.Copy,
                        [bt.reg] + R, R)
                    bk_ = nb()
                    bkb = bk_.ap.bitcast(BF16)
                    for dk in range(2):
                        TR(bk_, bkb[:, dk * 128:(dk + 1) * 128], kT[:, dk, tsl], identb, R + [CR])
                    V_ts(kd, bkb[:, 0:256], cols[:, 1:2], ALU.mult, [bk_.reg] + R, R)
                    for dk in range(2):
                        bC = nb()
                        MM(bC, bC.ap[:, :], kd[:, dk * 128:(dk + 1) * 128], v_tm[:, c, :], R, True, True)
                        V_stt(C32[:, dk, :], C32[:, dk, :], bcs[:, 128:129], bC.ap[:, :], ALU.mult, ALU.add,
                              [bC.reg] + R, R)
                    A_f(Cb, C32, AF.Copy, R, R)
                    bN = nb()
                    for dk in range(2):
                        B.op(PE, lambda: nc.tensor.matmul(bN.ap[:, dk:dk + 1], kd[:, dk * 128:(dk + 1) * 128], onesb[:, 0:1],
                                                          start=True, stop=True), R + [CR], [bN.reg])
                    V_stt(n32[:, j, h * 2:h * 2 + 2], n32[:, j, h * 2:h * 2 + 2], bcs[:, 128:129], bN.ap[:, 0:2],
                          ALU.mult, ALU.add, [bN.reg, nR] + R, [nR])
                B.dma(SP, o_p_C[j, h].rearrange("(k p) v -> p k v", p=128), C32, R, [PCr[jh]], stsem)
            ml_out_and_norm(l, j, N, hnT, sg)

        def mlstm_s(l, j):
            R = [AR]
            N = NS
            q_tm = av(0, 1024)[0:16]
            k_tm = av(4096, 1024)[0:16]
            v_tm = av(8192, 2048)[0:16]
            n0 = av(16384, 1024)[0:16]
            kw = av(20480, 1024)[0:16]
            kz = av(24576, 1024)[0:16]
            qT32 = av(28672, 128).rearrange("p (k n) -> p k n", k=8)
            qz = av(29184, 8 * 256).rearrange("p (k b c) -> p k b c", k=8, b=16)
            hnT = av(37376, KT * NS, BF16).rearrange("p (k n) -> p k n", k=KT)
            hno = av(37888, D, BF16)[0:16]
            tmpA = av(41984, 1024)[0:16]
            hs = av(46080, DV)[0:16]
            gbc = av(48128, DV)[0:16]
            SM = 50176
            def sm(k, n=4):
                return av(SM + 16 * k, n)[0:16]
            ig, zf, lf, m0, pp, mt, wts, scl, emt, qk, qn, sc, den, rr, tt4 = [sm(k) for k in range(15)]
            g8 = av(SM + 256, 8)[0:16]
            bif = av(SM + 288, 8)[0:16]
            D_ = av(SM + 320, 64)[0:16]
            scbc = av(SM + 576, 64)
            st6 = av(SM + 832, 6)[0:16]
            mv = av(SM + 856, 2)[0:16]
            wg = load_gate_w(j, SM + 1024)
            sg = av(SM + 2048 - 256 + 768, NS)
            Cbuf = [av(53248 + 4096 * k, 2 * DV).rearrange("p (k v) -> p k v", k=2) for k in range(2)]
            Cbr = [Reg("cbuf0"), Reg("cbuf1")]
            for r_ in Cbr:
                r_.w = AR.w
                r_.rs = dict(AR.rs)
            bg = nb()
            for kt in range(KT):
                MM(bg, bg.ap[0:16, 0:8], xb[:, kt, 0:16], wg[:, kt, :], [AR, xbr[kt]], kt == 0, kt == KT - 1)
            B.dma(SP, bif, ml_bif_tm[j], [], R, ldsem)
            B.dma(SP, m0, st_m[j], [], R, ldsem)
            B.dma(SP, n0, st_n[j], [], R, ldsem)
            V_tt(g8, bg.ap[0:16, 0:8], bif, ALU.add, [bg.reg] + R, R)
            V_cp(ig, g8[:, 0:4], R, R)
            V_cp(zf, g8[:, 4:8], R, R)
            log_sigmoid(lf, zf, tt4, 16, R)
            V_tt(pp, lf, m0, ALU.add, R, R)
            V_tt(mt, pp, ig, ALU.max, R, R)
            V_tt(wts, ig, mt, ALU.subtract, R, R)
            A_f(wts, wts, AF.Exp, R, R)
            V_tt(scl, pp, mt, ALU.subtract, R, R)
            A_f(scl, scl, AF.Exp, R, R)
            A_f(emt, mt, AF.Exp, R, R, scale=-1.0)
            B.dma(SP, o_s_m[j], mt, R, [], stsem)

            def evac_qT(ct, bs):
                A_f(qT32[:, ct, :], bs[0].ap[:, :N], AF.Copy, [bs[0].reg] + R, R)

            stream_fm([ml_win[j][:, 0:1024]], 1024, N, src_xb(N), evac_qT)

            def evac_tm(dst, scale):
                def f(cc, ti, bk):
                    A_f(dst[:, cc * 512:(cc + 1) * 512], bk.ap[0:16, :], AF.Copy, [bk.reg] + R, R, scale=scale)
                return f

            stream_tm(ml_win[j], 0, 2, [(0, 16)], evac_tm(q_tm, 1.0))
            stream_tm(ml_win[j], 1024, 2, [(0, 16)], evac_tm(k_tm, DK ** -0.5))
            stream_tm(ml_win[j], 2048, 4, [(0, 16)], evac_tm(v_tm, 1.0))
            V_tt(tmpA, q_tm, k_tm, ALU.mult, R, R)
            B.op(DVE, lambda: nc.vector.tensor_reduce(out=qk, in_=tmpA.rearrange("p (h d) -> p h d", h=4), axis=AX.X,
                                                      op=ALU.add), R, R)
            V_tt(tmpA, q_tm, n0, ALU.mult, R, R)
            B.op(DVE, lambda: nc.vector.tensor_reduce(out=qn, in_=tmpA.rearrange("p (h d) -> p h d", h=4), axis=AX.X,
                                                      op=ALU.add), R, R)
            V_tt(sc, qk, wts, ALU.mult, R, R)
            V_tt(den, scl, qn, ALU.mult, R, R)
            V_tt(den, den, sc, ALU.add, R, R)
            A_f(rr, den, AF.Abs, R, R)
            V_tt(rr, rr, emt, ALU.max, R, R)
            B.op(DVE, lambda: nc.vector.reciprocal(out=rr, in_=rr), R, R)
            k3 = k_tm.rearrange("p (h d) -> p h d", h=4)
            kw3 = kw.rearrange("p (h d) -> p h d", h=4)
            V_tt(kw3, k3, wts.unsqueeze(2).to_broadcast([16, 4, DK]), ALU.mult, R, R)
            t3 = tmpA.rearrange("p (h d) -> p h d", h=4)
            V_tt(t3, n0.rearrange("p (h d) -> p h d", h=4), scl.unsqueeze(2).to_broadcast([16, 4, DK]), ALU.mult, R, R)
            V_tt(tmpA, tmpA, kw, ALU.add, R, R)
            B.dma(SP, o_s_n[j], tmpA, R, [], stsem)
            B.op(DVE, lambda: nc.vector.memset(qz, 0.0), R, R)
            for b in range(NS):
                V_cp(qz[:, :, b, b:b + 1], qT32[:, :, b:b + 1], R, R)
            V_tt(D_.rearrange("p (b h) -> p b h", b=16), I16.unsqueeze(2).to_broadcast([16, 16, 4]),
                 scl.unsqueeze(1).to_broadcast([16, 16, 4]), ALU.mult, R + [CR], R)
            bsb = nb()
            B.op(PE, lambda: nc.tensor.matmul(bsb.ap[:, 0:64], consts[0:16, C_ONES:C_ONES + 128], D_, start=True, stop=True),
                 R + [CR], [bsb.reg])
            V_cp(scbc, bsb.ap[:, 0:64], [bsb.reg] + R, R)
            qbanks = []
            for h in range(NH):
                bk = nb()
                reserved.add(banks.index(bk))
                qbanks.append(bk)
            cnt = 0
            for b in range(NS):
                V_ts(kz, kw, I16[:, b:b + 1], ALU.mult, R + [CR], R)
                for h in range(NH):
                    cbuf = Cbuf[cnt % 2]
                    cr = Cbr[cnt % 2]
                    cnt += 1
                    B.dma(GQ, cbuf, st_C[j, b, h].rearrange("(k p) v -> p k v", p=128), [], [cr], csem)
                    for dk in range(2):
                        B.op(PE, lambda: nc.tensor.matmul(qbanks[h].ap[0:16, :], qz[:, h * 2 + dk, b, :], cbuf[:, dk, :],
                                                          start=(b == 0 and dk == 0), stop=(b == NS - 1 and dk == 1)),
                             [cr, AR], [qbanks[h].reg], sig=True)
                    for dk in range(2):
                        bo = nb()
                        B.op(PE, lambda: nc.tensor.matmul(bo.ap[:, :], kz[:, h * DK + dk * 128:h * DK + (dk + 1) * 128],
                                                          v_tm[:, h * DV:(h + 1) * DV], start=True, stop=True),
                             [AR], [bo.reg])
                        V_stt(cbuf[:, dk, :], cbuf[:, dk, :], scbc[:, b * 4 + h:b * 4 + h + 1], bo.ap[:, :], ALU.mult, ALU.add,
                              [bo.reg, cr, AR], [cr])
                    B.dma(SP, o_s_C[j, b, h].rearrange("(k p) v -> p k v", p=128), cbuf, [cr], [], stsem)
            for h in range(NH):
                B.dma(SP, gbc, ml_gbc[j][0:16, h * DV:(h + 1) * DV], [], R, ldsem)
                V_ts(hs, v_tm[:, h * DV:(h + 1) * DV], sc[:, h:h + 1], ALU.mult, R, R)
                V_stt(hs, qbanks[h].ap[0:16, :], scl[:, h:h + 1], hs, ALU.mult, ALU.add, [qbanks[h].reg] + R, R)
                V_ts(hs, hs, rr[:, h:h + 1], ALU.mult, R, R)
                head_norm(hs, 16, gbc, hno[:, h * DV:(h + 1) * DV], st6, mv, R)
                reserved.discard(banks.index(qbanks[h]))
            bt = nb()
            btb = bt.ap.bitcast(BF16)
            for kt in range(KT):
                TR(bt, btb[:, kt * 16:(kt + 1) * 16], hno[:, kt * 128:(kt + 1) * 128], identb[0:16, 0:16], R + [CR])
            A_f(hnT, btb[:, 0:KT * 16].rearrange("p (k n) -> p k n", k=KT), AF.Copy, [bt.reg] + R, R)
            collect_arena(Cbr)
            ml_out_and_norm(l, j, N, hnT, sg)

        tiles = [("p", t) for t in range(SEQ // NP)] + [("s", 0)]
        if dbg >= 300:
            tiles = tiles[:dbg - 300] + tiles[-1:]
            dbg = 0
        elif dbg >= 200:
            tiles = tiles[dbg - 200:]
            dbg = 0
        elif dbg >= 100:
            tiles = tiles[:dbg - 100]
            dbg = 0
        elif dbg:
            tiles = [("p", 0)]
        for kind, t in tiles:
            is_s = kind == "s"
            N = NS if is_s else NP
            wmode[0] = "store" if (tiles[0] == (kind, t)) and not is_s else ("load" if tiles[0][0] == "p" else "none")
            if is_s:
                B.dma(SP, x32[:, :, :N], xT_s.rearrange("(k p) t -> p k t", p=128), [], x32r, ldsem)
            else:
                B.dma(SP, x32[:, :, :N], xT_p[:, t * NP:(t + 1) * NP].rearrange("(k p) t -> p k t", p=128), [], x32r, ldsem)
            A_f(xb[:, :, :N], x32[:, :, :N], AF.Copy, x32r, xbr)
            stage = 0
            for l in range(DEPTH):
                j = l // 2
                ffn(l, 0, N)
                stage += 1
                if dbg and stage >= dbg:
                    break
                if l % 2 == 0:
                    s5_mix(l, j, N, is_s)
                elif is_s:
                    mlstm_s(l, j)
                else:
                    mlstm_p(l, j, N, t == 0)
                stage += 1
                if dbg and stage >= dbg:
                    break
                ffn(l, 1, N)
                stage += 1
                if dbg and stage >= dbg:
                    break
            if is_s:
                B.dma(SP, yT_s.rearrange("(k p) t -> p k t", p=128), x32[:, :, :N], x32r, [], stsem)
            else:
                B.dma(SP, yT_p[:, t * NP:(t + 1) * NP].rearrange("(k p) t -> p k t", p=128), x32[:, :, :N], x32r, [], stsem)
            if kind == "p" and t == SEQ // NP - 1:
                for j in range(2):
                    for r_ in range(2):
                        B.dma(SP, o_p_s5[j, r_], hst[:, j, r_, :], [hstR], [], stsem)
                    B.dma(SP, o_p_n[j], n32[:, j, :], [nR], [], stsem)
                B.dma(SP, o_p_m.rearrange("j h o -> o (j h)"), mrow[:, :], [mR], [], stsem)
        nc.sync.wait_ge(stsem.h, stsem.n)
        if scsem.n:
            nc.sync.wait_ge(scsem.h, scsem.n)
        for e in (PE, ACT, DVE):
            nc.sync.wait_ge(e.sem.h, e.sem.n)
    return nc


_CACHE = {}


def _consts():
    c = np.zeros((128, CW), np.float32)
    c[:, C_ID:C_ID + 128] = np.eye(128, dtype=np.float32)
    s = np.arange(128)[:, None]
    t = np.arange(128)[None, :]
    c[:, C_MASKT:C_MASKT + 128] = (s <= t).astype(np.float32)
    c[:, C_IOTA:C_IOTA + 512] = np.arange(1, 513, dtype=np.float32)[None, :]
    for h in range(4):
        c[h, C_SEL + h * 128:C_SEL + (h + 1) * 128] = 1.0
    c[0:16, C_I16:C_I16 + 16] = np.eye(16, dtype=np.float32)
    p = np.arange(128)
    for g2 in range(2):
        c[:, C_MG2 + g2] = ((p // 16) % 2 == g2).astype(np.float32)
        c[:, C_MH + g2] = ((p // 64) == g2).astype(np.float32)
    c[:, C_ONES:C_ONES + 128] = 1.0
    c[:, C_M96] = (p >= 96).astype(np.float32)
    return c


def kernel(x_prompt, x_sample, state_s5_re, state_s5_im, state_mlstm_C, state_mlstm_n, state_mlstm_m,
           ln_g, ln_b, ffn_w_gate, ffn_w_up, ffn_w_down,
           s5_a_re, s5_a_im, s5_log_dt, s5_b_re, s5_b_im, s5_c_re, s5_c_im, s5_d, s5_w_a, s5_w_b,
           ml_w_in, ml_b_i, ml_b_f, ml_norm_g, ml_w_out):
    f = lambda a: np.ascontiguousarray(np.asarray(a, dtype=np.float32))
    x_prompt, x_sample = f(x_prompt), f(x_sample)
    if "nc" not in _CACHE:
        _CACHE["nc"] = build_program()
    nc = _CACHE["nc"]

    def fm(v):
        v = f(v).reshape(-1, KT, 128)
        return np.ascontiguousarray(v.transpose(2, 0, 1).reshape(128, -1))

    def e_lay(a):
        a = np.repeat(f(a), 16, axis=1)
        return np.ascontiguousarray(a.reshape(2, KT, 128, -1).transpose(0, 2, 1, 3))

    def h_lay(a):
        return np.ascontiguousarray(f(a).reshape(2, 64, 2, PS).transpose(0, 2, 3, 1).reshape(2, 128, 64))

    shared = {
        "lng": fm(ln_g), "lnb": fm(ln_b),
        "w_gate": f(ffn_w_gate), "w_up": f(ffn_w_up), "w_down": f(ffn_w_down),
        "aE_re": e_lay(s5_a_re), "aE_im": e_lay(s5_a_im),
        "ldtE": np.ascontiguousarray(np.repeat(f(s5_log_dt), 16, axis=1).reshape(2, KT, 128).transpose(0, 2, 1)),
        "bE_re": np.ascontiguousarray(f(s5_b_re).transpose(0, 1, 3, 2).reshape(2, KT, 128, PS).transpose(0, 2, 1, 3)),
        "bE_im": np.ascontiguousarray(f(s5_b_im).transpose(0, 1, 3, 2).reshape(2, KT, 128, PS).transpose(0, 2, 1, 3)),
        "aH_re": h_lay(s5_a_re), "aH_im": h_lay(s5_a_im),
        "ldtH": np.ascontiguousarray(np.repeat(f(s5_log_dt).reshape(2, 64, 2, 1), PS, axis=3).transpose(0, 2, 3, 1).reshape(2, 128, 64)),
        "cH_re": np.ascontiguousarray(f(s5_c_re).reshape(2, 64, 2, 16, PS).transpose(0, 2, 4, 1, 3).reshape(2, 128, 64, 16)),
        "cH_im": np.ascontiguousarray(f(s5_c_im).reshape(2, 64, 2, 16, PS).transpose(0, 2, 4, 1, 3).reshape(2, 128, 64, 16)),
        "s5d": fm(s5_d), "s5_wa": f(s5_w_a), "s5_wb": f(s5_w_b),
        "ml_win": f(ml_w_in), "ml_wout": f(ml_w_out),
        "ml_bif0": np.ascontiguousarray(np.stack([f(ml_b_i), f(ml_b_f)], axis=-1).reshape(1, 16)),
        "ml_bif_tm": np.ascontiguousarray(np.broadcast_to(
            np.concatenate([f(ml_b_i), f(ml_b_f)], axis=1)[:, None, :], (2, 16, 8))),
        "ml_gbc": np.ascontiguousarray(np.broadcast_to(f(ml_norm_g)[:, None, :], (2, 128, D))),
        "consts": _consts(),
    }
    s5re, s5im = f(state_s5_re), f(state_s5_im)
    stC, stn, stm = f(state_mlstm_C), f(state_mlstm_n), f(state_mlstm_m)
    in_maps = []
    for c in range(N_CORES):
        bs = slice(NS * c, NS * (c + 1))
        m = dict(shared)
        m["xT_p"] = np.ascontiguousarray(x_prompt[c % 4].T)
        m["xT_s"] = np.ascontiguousarray(x_sample[bs, 0, :].T)
        def sl(a):
            return a[:, bs].reshape(2, NS, 64, 2, PS).transpose(0, 3, 4, 2, 1).reshape(2, 128, 64, NS)
        m["st_s5"] = np.ascontiguousarray(np.stack([sl(s5re), sl(s5im)], axis=1))
        m["st_C"] = np.ascontiguousarray(stC[:, bs])
        m["st_n"] = np.ascontiguousarray(stn[:, bs].reshape(2, NS, NH * DK))
        m["st_m"] = np.ascontiguousarray(stm[:, bs])
        in_maps.append(m)
    if _CACHE.get("prep_only"):
        return in_maps
    res = run_bass_kernel_spmd(nc, in_maps, core_ids=list(range(N_CORES)))
    rs = res.results
    y_prompt = np.stack([rs[b]["yT_p"].T for b in range(4)]).astype(np.float32)
    y_sample = np.concatenate([rs[c]["yT_s"].T[:, None, :] for c in range(N_CORES)], axis=0).astype(np.float32)

    def unh(a):
        lead = a.shape[:-2]
        return a.reshape(lead + (2, PS, 64)).transpose(tuple(range(len(lead))) + (len(lead) + 2, len(lead), len(lead) + 1)).reshape(lead + (G, PS))

    p_s5 = np.stack([rs[b]["o_p_s5"] for b in range(4)], axis=2)
    p_s5_re = np.ascontiguousarray(unh(p_s5[:, 0]))
    p_s5_im = np.ascontiguousarray(unh(p_s5[:, 1]))
    p_C = np.ascontiguousarray(np.stack([rs[b]["o_p_C"] for b in range(4)], axis=1))
    p_n = np.ascontiguousarray(np.stack(
        [rs[b]["o_p_n"].reshape(2, 128, NH, 2).transpose(0, 2, 3, 1).reshape(2, NH, DK) for b in range(4)], axis=1))
    p_m = np.ascontiguousarray(np.stack([rs[b]["o_p_m"].reshape(2, NH) for b in range(4)], axis=1))
    s_s5 = np.concatenate([rs[c]["o_s_s5"] for c in range(N_CORES)], axis=4)
    s_s5 = s_s5.reshape(2, 2, 2, PS, 64, NS * N_CORES).transpose(0, 1, 5, 4, 2, 3).reshape(2, 2, NS * N_CORES, G, PS)
    s_s5_re = np.ascontiguousarray(s_s5[:, 0])
    s_s5_im = np.ascontiguousarray(s_s5[:, 1])
    s_C = np.ascontiguousarray(np.concatenate([rs[c]["o_s_C"] for c in range(N_CORES)], axis=1))
    s_n = np.ascontiguousarray(np.concatenate([rs[c]["o_s_n"] for c in range(N_CORES)], axis=1).reshape(2, NS * N_CORES, NH, DK))
    s_m = np.ascontiguousarray(np.concatenate([rs[c]["o_s_m"] for c in range(N_CORES)], axis=1))
    return (y_prompt, y_sample, p_s5_re, p_s5_im, p_C, p_n, p_m, s_s5_re, s_s5_im, s_C, s_n, s_m)
```
